# Optimizing a Trainium2 kernel written in Bass

```python
import math
import jax, jax.numpy as jnp
from jax import lax
import numpy as np

D_MODEL = 2048
BATCH = 16
SEQ = 2048
DEPTH = 2
DEC_BATCH = 2
DEC_SEQ = 8192
PAST_LEN = 128

GRID_W = 64
N_MEM = 256
X_HEADS = 4
X_HEAD_DIM = D_MODEL // X_HEADS
GLA_HEADS = 4
GLA_DK = D_MODEL // 2
GLA_DV = D_MODEL
GLA_HEAD_K = GLA_DK // GLA_HEADS
GLA_HEAD_V = GLA_DV // GLA_HEADS
GATE_RANK = 16
GATE_NORM = 16.0
CHUNK = 64
GLA_IN = 2 * GLA_DK + 2 * GLA_DV + 2 * GATE_RANK
HEAD_DIM = 128
N_Q_HEADS = D_MODEL // HEAD_DIM
N_KV_HEADS = N_Q_HEADS // 4
GQA_GROUP = N_Q_HEADS // N_KV_HEADS
KV_DIM = N_KV_HEADS * HEAD_DIM
AXIS_DIM = HEAD_DIM // 2
ROPE_THETA = 10000.0
Q_BLOCK = 128
D_FF = 4 * D_MODEL
N_GLA = (DEPTH + 1) // 2
N_ATT = DEPTH // 2
DN_ALPHA = (2.0 * DEPTH) ** 0.25
DN_BETA = (8.0 * DEPTH) ** -0.25
EPS = 1e-5

kernel_name = "hybrid_gla_gqa_encoder"


def layer_norm(x, g, b):
    xf = x.astype(jnp.float32)
    mu = jnp.mean(xf, -1, keepdims=True)
    var = jnp.mean(jnp.square(xf - mu), -1, keepdims=True)
    return ((xf - mu) * lax.rsqrt(var + EPS) * g.astype(jnp.float32) + b.astype(jnp.float32)).astype(x.dtype)


def rms_norm(x, g):
    xf = x.astype(jnp.float32)
    return (xf * lax.rsqrt(jnp.mean(xf * xf, -1, keepdims=True) + EPS) * g.astype(jnp.float32)).astype(x.dtype)


def gla_scan(q, k, v, g, strict):
    B, H, N, dk = q.shape
    dv = v.shape[-1]
    nc = N // CHUNK
    rs = lambda t: t.reshape(B, H, nc, CHUNK, t.shape[-1])
    q, k, v, g = rs(q), rs(k), rs(v), rs(g)
    bcum = jnp.cumsum(g, axis=3)
    blast = bcum[:, :, :, -1:, :]
    qe = q * jnp.exp(bcum)
    ke = k * jnp.exp(-bcum)
    kd = k * jnp.exp(blast - bcum)
    mask = jnp.tril(jnp.ones((CHUNK, CHUNK), dtype=bool), -1 if strict else 0)
    a = jnp.where(mask, jnp.einsum('bhcid,bhcjd->bhcij', qe, ke), 0.0)
    o_intra = jnp.einsum('bhcij,bhcjv->bhciv', a, v)

    def step(S, inp):
        qe_c, kd_c, v_c, dec_c = inp
        o = jnp.einsum('bhid,bhdv->bhiv', qe_c, S)
        S = S * dec_c[..., None] + jnp.einsum('bhjd,bhjv->bhdv', kd_c, v_c)
        return S, o

    mv = lambda t: jnp.moveaxis(t, 2, 0)
    S0 = jnp.zeros((B, H, dk, dv), jnp.float32)
    _, o_inter = lax.scan(step, S0, (mv(qe), mv(kd), mv(v), mv(jnp.exp(blast[:, :, :, 0, :]))))
    o = o_intra + jnp.moveaxis(o_inter, 0, 2)
    return o.reshape(B, H, N, dv)


def gla_mixer(x, w_in, w_gate_up, b_gate, norm_g, w_out):
    B, N, _ = x.shape
    proj = x @ w_in
    q, k, v, r, zf, zb = jnp.split(
        proj, [GLA_DK, 2 * GLA_DK, 2 * GLA_DK + GLA_DV, 2 * GLA_DK + 2 * GLA_DV,
               2 * GLA_DK + 2 * GLA_DV + GATE_RANK], axis=-1)

    def heads(t, hd):
        return t.reshape(B, N, GLA_HEADS, hd).transpose(0, 2, 1, 3).astype(jnp.float32)

    q = heads(q, GLA_HEAD_K) * (GLA_HEAD_K ** -0.5)
    k = heads(k, GLA_HEAD_K)
    v = heads(v, GLA_HEAD_V)
    gf = jax.nn.log_sigmoid((zf @ w_gate_up[0] + b_gate[0]).astype(jnp.float32)) / GATE_NORM
    gb = jax.nn.log_sigmoid((zb @ w_gate_up[1] + b_gate[1]).astype(jnp.float32)) / GATE_NORM
    gf = heads(gf, GLA_HEAD_K)
    gb = heads(gb, GLA_HEAD_K)
    o_f = gla_scan(q, k, v, gf, strict=False)
    flip = lambda t: jnp.flip(t, axis=2)
    o_b = flip(gla_scan(flip(q), flip(k), flip(v), flip(gb), strict=True))
    o = (o_f + o_b).astype(x.dtype)
    o = rms_norm(o, norm_g)
    o = o.transpose(0, 2, 1, 3).reshape(B, N, GLA_DV)
    o = o * jax.nn.silu(r)
    return o @ w_out


def axial_rope(n):
    rows = n // GRID_W
    row = jnp.repeat(jnp.arange(rows, dtype=jnp.float32), GRID_W)
    col = jnp.tile(jnp.arange(GRID_W, dtype=jnp.float32), rows)
    inv = ROPE_THETA ** (-jnp.arange(0, AXIS_DIM, 2, dtype=jnp.float32) / AXIS_DIM)
    ar = row[:, None] * inv
    ac = col[:, None] * inv
    ang = jnp.concatenate([ar, ar, ac, ac], axis=-1)
    return jnp.cos(ang), jnp.sin(ang)


def rotate_half(t):
    t1, t2 = jnp.split(t, 2, axis=-1)
    return jnp.concatenate([-t2, t1], axis=-1)


def apply_axial_rope(x, cos, sin):
    xr, xc = jnp.split(x, 2, axis=-1)
    rot = jnp.concatenate([rotate_half(xr), rotate_half(xc)], axis=-1)
    c, s = cos[:, None, :], sin[:, None, :]
    return (x.astype(jnp.float32) * c + rot.astype(jnp.float32) * s).astype(x.dtype)


def gqa_mixer(x, w_qkv, q_gain, k_gain, w_out):
    B, N, _ = x.shape
    qkv = x @ w_qkv
    q, k, v = jnp.split(qkv, [D_MODEL, D_MODEL + KV_DIM], axis=-1)
    q = rms_norm(q.reshape(B, N, N_Q_HEADS, HEAD_DIM), q_gain)
    k = rms_norm(k.reshape(B, N, N_KV_HEADS, HEAD_DIM), k_gain)
    v = v.reshape(B, N, N_KV_HEADS, HEAD_DIM)
    cos, sin = axial_rope(N)
    q = apply_axial_rope(q, cos, sin)
    k = apply_axial_rope(k, cos, sin)
    nb = N // Q_BLOCK
    qb = q.reshape(B, nb, Q_BLOCK, N_KV_HEADS, GQA_GROUP, HEAD_DIM).swapaxes(0, 1)
    scale = HEAD_DIM ** -0.5

    def attend(qblk):
        s = jnp.einsum('bqkgd,bskd->bkgqs', qblk, k).astype(jnp.float32) * scale
        p = jax.nn.softmax(s, axis=-1)
        return jnp.einsum('bkgqs,bskd->bqkgd', p.astype(v.dtype), v)

    o = lax.map(attend, qb)
    o = o.swapaxes(0, 1).reshape(B, N, D_MODEL)
    return o @ w_out


def mem_cross_attn(x, mem, w_q, w_kv, w_o):
    B, N, _ = x.shape
    M = mem.shape[1]
    q = (x @ w_q).reshape(B, N, X_HEADS, X_HEAD_DIM)
    k, v = jnp.split(mem @ w_kv, 2, axis=-1)
    k = k.reshape(B, M, X_HEADS, X_HEAD_DIM)
    v = v.reshape(B, M, X_HEADS, X_HEAD_DIM)
    s = jnp.einsum('bnhd,bmhd->bhnm', q, k).astype(jnp.float32) * (X_HEAD_DIM ** -0.5)
    p = jax.nn.softmax(s, axis=-1)
    o = jnp.einsum('bhnm,bmhd->bnhd', p.astype(v.dtype), v).reshape(B, N, D_MODEL)
    return o @ w_o


def sq_relu_mlp(x, w1, w2):
    return jnp.square(jax.nn.relu(x @ w1)) @ w2


def trunk(x, mem, gla_w_in, gla_w_gate_up, gla_b_gate, gla_norm_g, gla_w_out,
          att_w_qkv, att_q_gain, att_k_gain, att_w_out,
          mem_w_q, mem_w_kv, mem_w_o, mlp_w1, mlp_w2, ln_g, ln_b):
    for i in range(DEPTH):
        j = i // 2
        if i % 2 == 0:
            h = gla_mixer(x, gla_w_in[j], gla_w_gate_up[j], gla_b_gate[j], gla_norm_g[j], gla_w_out[j])
        else:
            h = gqa_mixer(x, att_w_qkv[j], att_q_gain[j], att_k_gain[j], att_w_out[j])
        x = layer_norm(DN_ALPHA * x + h, ln_g[i, 0], ln_b[i, 0])
        x = layer_norm(DN_ALPHA * x + mem_cross_attn(x, mem, mem_w_q[i], mem_w_kv[i], mem_w_o[i]),
                       ln_g[i, 1], ln_b[i, 1])
        x = layer_norm(DN_ALPHA * x + sq_relu_mlp(x, mlp_w1[i], mlp_w2[i]), ln_g[i, 2], ln_b[i, 2])
    return x


def setup_inputs(seed: int = 0) -> dict:
    key = jax.random.key(seed)
    ks = jax.random.split(key, 24)
    f32 = jnp.float32
    nrm = lambda k, shape, scale: jax.random.normal(k, shape, f32) * scale
    x_prompt = nrm(ks[0], (BATCH, SEQ, D_MODEL), 1.0)
    x_sample = nrm(ks[1], (DEC_BATCH, DEC_SEQ, D_MODEL), 1.0)
    mem_prompt = nrm(ks[2], (BATCH, N_MEM, D_MODEL), 1.0)
    mem_sample = nrm(ks[3], (DEC_BATCH, N_MEM, D_MODEL), 1.0)
    fan = D_MODEL ** -0.5
    col_scale = jnp.concatenate([
        jnp.ones((2 * GLA_DK,), f32), jnp.full((GLA_DV,), DN_BETA, f32),
        jnp.ones((GLA_DV + 2 * GATE_RANK,), f32)])
    gla_w_in = nrm(ks[4], (N_GLA, D_MODEL, GLA_IN), fan) * col_scale
    gla_w_gate_up = nrm(ks[5], (N_GLA, 2, GATE_RANK, GLA_DK), GATE_RANK ** -0.5)
    gla_b_gate = nrm(ks[6], (N_GLA, 2, GLA_DK), 0.1)
    gla_norm_g = 1.0 + nrm(ks[7], (N_GLA, GLA_HEAD_V), 0.01)
    gla_w_out = nrm(ks[8], (N_GLA, GLA_DV, D_MODEL), GLA_DV ** -0.5 * DN_BETA)
    qkv_scale = jnp.concatenate([jnp.ones((D_MODEL + KV_DIM,), f32), jnp.full((KV_DIM,), DN_BETA, f32)])
    att_w_qkv = nrm(ks[9], (N_ATT, D_MODEL, D_MODEL + 2 * KV_DIM), fan) * qkv_scale
    att_q_gain = 1.0 + nrm(ks[10], (N_ATT, HEAD_DIM), 0.01)
    att_k_gain = 1.0 + nrm(ks[11], (N_ATT, HEAD_DIM), 0.01)
    att_w_out = nrm(ks[12], (N_ATT, D_MODEL, D_MODEL), fan * DN_BETA)
    mem_w_q = nrm(ks[13], (DEPTH, D_MODEL, D_MODEL), fan)
    kv_scale = jnp.concatenate([jnp.ones((D_MODEL,), f32), jnp.full((D_MODEL,), DN_BETA, f32)])
    mem_w_kv = nrm(ks[14], (DEPTH, D_MODEL, 2 * D_MODEL), fan) * kv_scale
    mem_w_o = nrm(ks[15], (DEPTH, D_MODEL, D_MODEL), fan * DN_BETA)
    mlp_w1 = nrm(ks[16], (DEPTH, D_MODEL, D_FF), fan)
    mlp_w2 = nrm(ks[17], (DEPTH, D_FF, D_MODEL), D_FF ** -0.5 * DN_BETA)
    ln_g = 1.0 + nrm(ks[18], (DEPTH, 3, D_MODEL), 0.01)
    ln_b = nrm(ks[19], (DEPTH, 3, D_MODEL), 0.01)
    return {"x_prompt": x_prompt, "x_sample": x_sample, "mem_prompt": mem_prompt, "mem_sample": mem_sample,
            "gla_w_in": gla_w_in, "gla_w_gate_up": gla_w_gate_up, "gla_b_gate": gla_b_gate,
            "gla_norm_g": gla_norm_g, "gla_w_out": gla_w_out,
            "att_w_qkv": att_w_qkv, "att_q_gain": att_q_gain, "att_k_gain": att_k_gain, "att_w_out": att_w_out,
            "mem_w_q": mem_w_q, "mem_w_kv": mem_w_kv, "mem_w_o": mem_w_o,
            "mlp_w1": mlp_w1, "mlp_w2": mlp_w2, "ln_g": ln_g, "ln_b": ln_b}


def reference(x_prompt, x_sample, mem_prompt, mem_sample,
              gla_w_in, gla_w_gate_up, gla_b_gate, gla_norm_g, gla_w_out,
              att_w_qkv, att_q_gain, att_k_gain, att_w_out,
              mem_w_q, mem_w_kv, mem_w_o, mlp_w1, mlp_w2, ln_g, ln_b):
    y_prompt = trunk(x_prompt, mem_prompt, gla_w_in, gla_w_gate_up, gla_b_gate, gla_norm_g, gla_w_out,
                     att_w_qkv, att_q_gain, att_k_gain, att_w_out,
                     mem_w_q, mem_w_kv, mem_w_o, mlp_w1, mlp_w2, ln_g, ln_b)
    y_sample = trunk(x_sample, mem_sample, gla_w_in, gla_w_gate_up, gla_b_gate, gla_norm_g, gla_w_out,
                     att_w_qkv, att_q_gain, att_k_gain, att_w_out,
                     mem_w_q, mem_w_kv, mem_w_o, mlp_w1, mlp_w2, ln_g, ln_b)
    return (y_prompt, y_sample)
```

```python
import contextlib
import numpy as np
import concourse.bass as bass
import concourse.mybir as mybir
from concourse.bass_utils import run_bass_kernel_spmd

F32 = mybir.dt.float32
BF16 = mybir.dt.bfloat16
AF = mybir.ActivationFunctionType
ALU = mybir.AluOpType
AX = mybir.AxisListType

NDMA = 56
SAME_ENGINE_SYNC = True

D = 2048
KC = 16
NMEM = 256
DFF = 8192
EPS = 1e-5
DEPTH = 2
DN_ALPHA = (2.0 * DEPTH) ** 0.25
GIN = 6176


class _Op:
    __slots__ = ("eng", "fn", "deps", "is_dma", "slot", "dval", "pos", "sig", "sigval", "waits", "gw")


class Sched:
    CE = ("pe", "act", "dve", "pool")

    def __init__(self, nc, stack):
        self.nc = nc
        self.esem = {e: stack.enter_context(nc.semaphore("s_" + e)) for e in self.CE}
        self.ecount = {e: 0 for e in self.CE}
        self.dsem = [stack.enter_context(nc.semaphore("d%d" % i)) for i in range(NDMA)]
        self.dcount = [0] * NDMA
        self.carry = []
        self.gk = {}
        self.gslots = {}
        self._reset_phase()
        self.n_inst = 0

    def _reset_phase(self):
        self.ops = []
        self.last_w = {}
        self.readers = {}
        self.slotmap = {}

    def _deps(self, reads, writes):
        deps = set()
        for k in reads:
            w = self.last_w.get(k)
            if w is not None:
                deps.add(w)
        for k in writes:
            w = self.last_w.get(k)
            if w is not None:
                deps.add(w)
            for r in self.readers.get(k, ()):
                deps.add(r)
        return deps

    def _record(self, idx, reads, writes):
        for k in reads:
            self.readers.setdefault(k, []).append(idx)
        for k in writes:
            self.last_w[k] = idx
            self.readers[k] = []

    def _gw(self, reads):
        return [self.gk[k] for k in reads if (k in self.gk and k not in self.last_w)]

    def op(self, eng, fn, reads=(), writes=()):
        o = _Op()
        o.eng = eng
        o.fn = fn
        o.is_dma = False
        o.slot = None
        o.dval = 0
        o.sig = False
        o.gw = self._gw(reads)
        o.deps = self._deps(reads, writes)
        idx = len(self.ops)
        self.ops.append(o)
        self._record(idx, reads, writes)
        return idx

    def dma(self, queue, fn, reads=(), writes=(), semkey=None, gkey=None):
        o = _Op()
        o.eng = queue
        o.fn = fn
        o.is_dma = True
        if gkey is not None:
            if gkey not in self.gslots:
                self.gslots[gkey] = NDMA - 1 - len(self.gslots)
            o.slot = self.gslots[gkey]
        else:
            if semkey not in self.slotmap:
                self.slotmap[semkey] = len(self.slotmap)
                assert len(self.slotmap) + len(self.gslots) <= NDMA, "too many dma sem keys in phase"
            o.slot = self.slotmap[semkey]
        self.dcount[o.slot] += 16
        o.dval = self.dcount[o.slot]
        if gkey is not None:
            self.gk[gkey] = (o.slot, o.dval)
        o.sig = False
        o.gw = self._gw(reads)
        o.deps = self._deps(reads, writes)
        idx = len(self.ops)
        self.ops.append(o)
        self._record(idx, reads, writes)
        return idx

    def end_phase(self):
        ops = self.ops
        engs = ("pe", "act", "dve", "pool", "sp")
        pos_ctr = {e: 0 for e in engs}
        for o in ops:
            o.pos = pos_ctr[o.eng]
            pos_ctr[o.eng] += 1
        waited_pos = {}
        waited_dma = {}
        for o in ops:
            X = o.eng
            o.waits = []
            for (gs, gv) in o.gw:
                if waited_dma.get((X, gs), 0) >= gv:
                    continue
                waited_dma[(X, gs)] = gv
                o.waits.append(("d", gs, gv))
            for d in sorted(o.deps):
                Dd = ops[d]
                if Dd.is_dma:
                    if waited_dma.get((X, Dd.slot), 0) >= Dd.dval:
                        continue
                    waited_dma[(X, Dd.slot)] = Dd.dval
                    o.waits.append(("d", Dd.slot, Dd.dval))
                else:
                    Y = Dd.eng
                    if Y == X and (X == "pe" or not SAME_ENGINE_SYNC):
                        continue
                    if waited_pos.get((X, Y), -1) >= Dd.pos:
                        continue
                    waited_pos[(X, Y)] = Dd.pos
                    Dd.sig = True
                    o.waits.append(("e", Y, d))
        last = {}
        for i, o in enumerate(ops):
            if not o.is_dma:
                last[o.eng] = i
        for e, i in last.items():
            ops[i].sig = True
        for o in ops:
            if (not o.is_dma) and o.sig:
                self.ecount[o.eng] += 1
                o.sigval = self.ecount[o.eng]
        carry = self.carry
        sched = self

        def stream(X):
            def body(eng):
                for (sem, val) in carry:
                    eng.wait_ge(sem, val)
                for o in ops:
                    if o.eng != X:
                        continue
                    for w in o.waits:
                        if w[0] == "d":
                            eng.wait_ge(sched.dsem[w[1]], w[2])
                        else:
                            eng.wait_ge(sched.esem[w[1]], ops[w[2]].sigval)
                    inst = o.fn(eng)
                    sched.n_inst += 1
                    if o.is_dma:
                        inst.then_inc(sched.dsem[o.slot], 16)
                    elif o.sig:
                        inst.then_inc(sched.esem[X], 1)
            return body

        with self.nc.Block() as block:
            block.tensor(stream("pe"))
            block.scalar(stream("act"))
            block.vector(stream("dve"))
            block.gpsimd(stream("pool"))
            block.sync(stream("sp"))
        newc = []
        for e in self.CE:
            if self.ecount[e] > 0:
                newc.append((self.esem[e], self.ecount[e]))
        gset = set(self.gslots.values())
        for s in range(NDMA):
            if self.dcount[s] > 0 and s not in gset:
                newc.append((self.dsem[s], self.dcount[s]))
        self.carry = newc
        self._reset_phase()


class Prog:
    def __init__(self, SEG, dbg=False):
        self.SEG = SEG
        self.TB = min(512, SEG)
        self.NT = self.TB // 128
        self.NBS = SEG // self.TB
        self.NSA = 6
        self.NSB = 3
        self.dbg = dbg
        self.nc = bass.Bass("TRN2", target_bir_lowering=False)
        self._uid = 0

    def din(self, name, shape, dt=F32):
        return self.nc.dram_tensor(name, list(shape), dt, kind="ExternalInput").ap()

    def dscr(self, name, shape, dt):
        return self.nc.dram_tensor(name, list(shape), dt, kind="Internal").ap()

    def dout(self, name, shape, dt=F32):
        return self.nc.dram_tensor(name, list(shape), dt, kind="ExternalOutput").ap()

    def sb(self, st, name, shape, dt):
        self._uid += 1
        return st.enter_context(self.nc.sbuf_tensor("%s_%d" % (name, self._uid), list(shape), dt))

    def ps(self, st, name, shape, dt=F32):
        self._uid += 1
        return st.enter_context(self.nc.psum_tensor("%s_%d" % (name, self._uid), list(shape), dt))

    def mmg(self, out, pairs, reads, writes, first=True, last=True):
        pairs = list(pairs)

        def fn(e):
            n = len(pairs)
            i = None
            for j, (l, r) in enumerate(pairs):
                i = e.matmul(out, lhsT=l, rhs=r, start=(first and j == 0), stop=(last and j == n - 1))
            return i
        return self.S.op("pe", fn, reads=reads, writes=writes)

    def declare(self):
        SEG, TB = self.SEG, self.TB
        NA = self.NSA * SEG
        NB_ = self.NSB * SEG
        i = {}
        i["xA"] = self.din("xA", [NA, D])
        i["memA"] = self.din("memA", [3 * NMEM, D])
        i["gla_w_in"] = self.din("gla_w_in", [D, GIN])
        i["gla_w_gate_up"] = self.din("gla_w_gate_up", [2, 16, 1024])
        i["gla_b_gate"] = self.din("gla_b_gate", [2, 1024])
        i["gla_norm_g"] = self.din("gla_norm_g", [1, 512])
        i["gla_w_out"] = self.din("gla_w_out", [D, D])
        i["att_w_qkv"] = self.din("att_w_qkv", [D, 3072])
        i["att_q_gain"] = self.din("att_q_gain", [1, 128])
        i["att_k_gain"] = self.din("att_k_gain", [1, 128])
        i["att_w_out"] = self.din("att_w_out", [D, D])
        i["mem_w_q"] = self.din("mem_w_q", [2, D, D])
        i["mem_w_kv"] = self.din("mem_w_kv", [2, D, 2 * D])
        i["mem_w_o"] = self.din("mem_w_o", [2, D, D])
        i["mlp_w1"] = self.din("mlp_w1", [2, D, DFF])
        i["mlp_w2"] = self.din("mlp_w2", [2, DFF, D])
        i["ln_g"] = self.din("ln_g", [2, 3, D])
        i["ln_b"] = self.din("ln_b", [2, 3, D])
        i["ident"] = self.din("ident", [128, 128])
        i["trif"] = self.din("trif", [128, 130])
        i["trib"] = self.din("trib", [128, 130])
        i["trikf"] = self.din("trikf", [128, 128])
        i["trikb"] = self.din("trikb", [128, 128])
        i["maskf"] = self.din("maskf", [128, 128])
        i["maskb"] = self.din("maskb", [128, 128])
        i["csA"] = self.din("csA", [4 * SEG, 2, 128])
        i["csOwn"] = self.din("csOwn", [SEG, 2, 128])
        i["ownmask"] = self.din("ownmask", [128, 4])
        self.i = i
        s = {}
        s["wb_gin"] = self.dscr("wb_gin", [D, GIN], BF16)
        s["wb_gout"] = self.dscr("wb_gout", [D, D], BF16)
        s["wb_qkv"] = self.dscr("wb_qkv", [D, 3072], BF16)
        s["wb_aout"] = self.dscr("wb_aout", [D, D], BF16)
        s["wb_mq"] = self.dscr("wb_mq", [2, D, D], BF16)
        s["wb_mkv"] = self.dscr("wb_mkv", [2, D, 2 * D], BF16)
        s["wb_mo"] = self.dscr("wb_mo", [2, D, D], BF16)
        s["wb_w1"] = self.dscr("wb_w1", [2, D, DFF], BF16)
        s["wb_w2"] = self.dscr("wb_w2", [2, DFF, D], BF16)
        NBA = NA // TB
        NBB = NB_ // TB
        for nm in ("XTa", "XTb", "OT"):
            s[nm] = self.dscr(nm, [NBA, 128, KC, TB], BF16)
        for nm in ("XFa", "XFb"):
            s[nm] = self.dscr(nm, [NA, D], F32)
        s["XT1"] = self.dscr("XT1", [NBB, 128, KC, TB], BF16)
        s["XF1"] = self.dscr("XF1", [NB_, D], F32)
        s["SBst"] = self.dscr("SBst", [4 * SEG // 64, 128, 2, 512], BF16)
        s["KVst"] = self.dscr("KVst", [4 * SEG // 128, 128, 768], BF16)
        s["MK"] = self.dscr("MK", [2, 128, KC, 3 * NMEM], BF16)
        s["MV"] = self.dscr("MV", [2, 128, 6, D], BF16)
        s["QT"] = self.dscr("QT", [NBB, 128, KC, TB], BF16)
        s["KT"] = self.dscr("KT", [128, 4, NA], BF16)
        s["VV"] = self.dscr("VV", [NA, 512], BF16)
        self.s = s
        self.y = self.dout("y", [NB_, D])
        if self.dbg:
            self.dbgout = self.dout("dbg", [NA, D])

    def phase_weights(self, part):
        S = self.S
        i, s = self.i, self.s

        def conv(dst, src, rows, key):
            cols = src.shape[-1]
            nch = max(1, rows // 256)
            rs = rows // nch
            for c in range(nch):
                if cols > 2048 and cols % 2048 == 0:
                    o_ = dst[c * rs:(c + 1) * rs, :].rearrange("r (a b) -> r a b", b=2048)
                    i_ = src[c * rs:(c + 1) * rs, :].rearrange("r (a b) -> r a b", b=2048)
                else:
                    o_ = dst[c * rs:(c + 1) * rs, :]
                    i_ = src[c * rs:(c + 1) * rs, :]
                kw = {} if cols <= 2048 or cols % 2048 == 0 else {"max_dma_last_dim": 4096}
                S.dma("pool", lambda e, o_=o_, i_=i_, kw=kw: e.dma_start(out=o_, in_=i_, **kw),
                      gkey=key)
        if part == 0:
            conv(s["wb_gin"], i["gla_w_in"], D, "wb_gin")
            for l in range(2):
                conv(s["wb_mkv"][l], i["mem_w_kv"][l], D, "wb_mkv")
            return
        conv(s["wb_gout"], i["gla_w_out"], D, "wb_gout")
        conv(s["wb_mq"][0], i["mem_w_q"][0], D, "wb_mq")
        conv(s["wb_mo"][0], i["mem_w_o"][0], D, "wb_mo")
        conv(s["wb_w1"][0], i["mlp_w1"][0], D, "wb_w1")
        conv(s["wb_w2"][0], i["mlp_w2"][0], DFF, "wb_w2")
        conv(s["wb_qkv"], i["att_w_qkv"], D, "wb_qkv")
        conv(s["wb_aout"], i["att_w_out"], D, "wb_aout")
        conv(s["wb_mq"][1], i["mem_w_q"][1], D, "wb_mq")
        conv(s["wb_mo"][1], i["mem_w_o"][1], D, "wb_mo")
        conv(s["wb_w1"][1], i["mlp_w1"][1], D, "wb_w1")
        conv(s["wb_w2"][1], i["mlp_w2"][1], DFF, "wb_w2")

    def transpose_tile(self, src_fn, src_keys, dst_blk, col, dst_key, idb, pT, ptkeys, ctr, nchunks=KC, dst_c0=0):
        S = self.S
        for g in range(0, nchunks, 8):
            n = min(8, nchunks - g)
            b = ctr[0] % len(ptkeys)
            ctr[0] += 1

            def tp(e, g=g, n=n, b=b):
                ins = None
                for k in range(n):
                    ins = e.transpose(out=pT[b][:, k, :], in_=src_fn(g + k), identity=idb[:])
                return ins
            S.op("pe", tp, reads=list(src_keys) + ["idb"], writes=[ptkeys[b]])
            eng = "act" if (ctr[0] % 2) else "dve"
            if eng == "act":
                S.op("act", lambda e, g=g, n=n, b=b: e.copy(out=dst_blk[:, dst_c0 + g:dst_c0 + g + n, col:col + 128], in_=pT[b][:, 0:n, :]),
                     reads=[ptkeys[b]], writes=[dst_key])
            else:
                S.op("dve", lambda e, g=g, n=n, b=b: e.tensor_copy(out=dst_blk[:, dst_c0 + g:dst_c0 + g + n, col:col + 128], in_=pT[b][:, 0:n, :]),
                     reads=[ptkeys[b]], writes=[dst_key])

    def load_consts(self, st, want):
        S = self.S
        c = {}
        if "ident" in want:
            idf = self.sb(st, "idf", [128, 128], F32)
            idb = self.sb(st, "idb", [128, 128], BF16)
            S.dma("sp", lambda e: e.dma_start(out=idf[:], in_=self.i["ident"]), writes=["idf"], semkey="idf")
            S.op("dve", lambda e: e.tensor_copy(out=idb[:], in_=idf[:]), reads=["idf"], writes=["idb"])
            c["idb"] = idb
        if "ones" in want:
            ones = self.sb(st, "ones", [128, 128], BF16)
            S.op("pool", lambda e: e.memset(ones[:], 1.0), writes=["ones"])
            c["ones"] = ones
        return c

    def phase_prep(self, xsrc, XT, ntok):
        S = self.S
        TB, NT = self.TB, self.NT
        with contextlib.ExitStack() as st:
            c = self.load_consts(st, ["ident"])
            idb = c["idb"]
            xin = [self.sb(st, "xin", [128, D], F32) for _ in range(2)]
            xbf = [self.sb(st, "xbf", [128, D], BF16) for _ in range(2)]
            blk = [self.sb(st, "blk", [128, KC, TB], BF16) for _ in range(2)]
            pT = [self.ps(st, "pT", [128, 8, 128], BF16) for _ in range(4)]
            ptk = [("pT", j) for j in range(4)]
            ctr = [0]
            for b in range(ntok // TB):
                bb = b % 2
                for t in range(NT):
                    g = b * NT + t
                    p = g % 2
                    S.dma("sp", lambda e, g=g, p=p: e.dma_start(out=xin[p][:], in_=xsrc[g * 128:(g + 1) * 128, :]),
                          writes=[("xin", p)], semkey=("xin", p))
                    S.op("act" if g % 2 else "dve",
                         (lambda e, p=p: e.copy(out=xbf[p][:], in_=xin[p][:])) if g % 2 else
                         (lambda e, p=p: e.tensor_copy(out=xbf[p][:], in_=xin[p][:])),
                         reads=[("xin", p)], writes=[("xbf", p)])
                    self.transpose_tile(lambda k, p=p: xbf[p][:, k * 128:(k + 1) * 128], [("xbf", p)], blk[bb], t * 128,
                                        ("blk", bb), idb, pT, ptk, ctr)
                S.dma("sp", lambda e, b=b, bb=bb: e.dma_start(out=XT[b], in_=blk[bb][:]),
                      reads=[("blk", bb)], writes=[("XT", b)], semkey=("blkst", bb))
            self.S.end_phase()

    def ln_tile(self, li, h, hkey, L):
        S = self.S
        p = li % 2
        pb = li % len(L["xob"])
        stats, mv, rstd = L["stats"][p], L["mv"][p], L["rstd"][p]
        px = li % len(L["xo"])
        xo, xob = L["xo"][px], L["xob"][pb]

        def bn(e):
            ins = None
            for j in range(4):
                ins = e.bn_stats(out=stats[:, j, :], in_=h[:, j * 512:(j + 1) * 512])
            return ins
        S.op("dve", bn, reads=[hkey], writes=[("stats", p)])
        S.op("dve", lambda e: e.bn_aggr(out=mv[:], in_=stats[:].rearrange("p a b -> p (a b)")),
             reads=[("stats", p)], writes=[("mv", p)])
        S.op("act", lambda e: e.activation(out=rstd[:], in_=mv[:, 1:2], func=AF.Ln, bias=L["epsc"][:], scale=1.0),
             reads=[("mv", p), "epsc"], writes=[("rstd", p)])
        S.op("act", lambda e: e.activation(out=rstd[:], in_=rstd[:], func=AF.Exp, scale=-0.5),
             reads=[("rstd", p)], writes=[("rstd", p)])
        S.op("dve", lambda e: e.scalar_tensor_tensor(out=xo[:], in0=h[:], scalar=mv[:, 0:1], in1=L["gbc"][:],
                                                     op0=ALU.subtract, op1=ALU.mult),
             reads=[hkey, ("mv", p), "gbc"], writes=[("xo", px)])
        S.op("dve", lambda e: e.scalar_tensor_tensor(out=xo[:], in0=xo[:], scalar=rstd[:], in1=L["bbc"][:],
                                                     op0=ALU.mult, op1=ALU.add),
             reads=[("xo", px), ("rstd", p), "bbc"], writes=[("xo", px)])
        S.op("act", lambda e: e.copy(out=xob[:], in_=xo[:]), reads=[("xo", px)], writes=[("xob", pb)])
        return px, pb

    def ln_alloc(self, st, layer, which, nxob=2, nxo=2):
        S = self.S
        L = {}
        L["stats"] = [self.sb(st, "stats", [128, 4, 6], F32) for _ in range(2)]
        L["mv"] = [self.sb(st, "mv", [128, 2], F32) for _ in range(2)]
        L["rstd"] = [self.sb(st, "rstd", [128, 1], F32) for _ in range(2)]
        L["xo"] = [self.sb(st, "xo", [128, D], F32) for _ in range(nxo)]
        L["xob"] = [self.sb(st, "xob", [128, D], BF16) for _ in range(nxob)]
        L["gbc"] = self.sb(st, "gbc", [128, D], F32)
        L["bbc"] = self.sb(st, "bbc", [128, D], F32)
        L["epsc"] = self.sb(st, "epsc", [128, 1], F32)
        S.op("pool", lambda e: e.memset(L["epsc"][:], EPS), writes=["epsc"])
        g = self.i["ln_g"][layer, which:which + 1, :]
        b = self.i["ln_b"][layer, which:which + 1, :]
        S.dma("sp", lambda e: e.dma_start(out=L["gbc"][:], in_=g.partition_broadcast(128)), writes=["gbc"], semkey="gbc")
        S.dma("sp", lambda e: e.dma_start(out=L["bbc"][:], in_=b.partition_broadcast(128)), writes=["bbc"], semkey="bbc")
        return L

    def phase_proj_ln(self, Wb, wkey, OT, XFin, XFout, XTout, ntok, layer, which, final_out=None):
        S = self.S
        TB, NT = self.TB, self.NT
        with contextlib.ExitStack() as st:
            c = self.load_consts(st, ["ident"])
            idb = c["idb"]
            L = self.ln_alloc(st, layer, which)
            W = self.sb(st, "W", [128, KC, D], BF16)
            for q in range(4):
                S.dma("sp", lambda e, q=q: e.dma_start(out=W[:, q * 4:(q + 1) * 4, :],
                                                       in_=Wb[q * 512:(q + 1) * 512, :].rearrange("(k p) n -> p k n", p=128)),
                      reads=[wkey], writes=[("W", q)], semkey=("W", q))
            ot = [self.sb(st, "ot", [128, KC, TB], BF16) for _ in range(2)]
            xf = [self.sb(st, "xf", [128, D], F32) for _ in range(2)]
            hb = [self.sb(st, "hb", [128, D], F32) for _ in range(2)]
            blk = [self.sb(st, "blk", [128, KC, TB], BF16) for _ in range(2)] if XTout is not None else None
            pY = self.ps(st, "pY", [128, 4, 512], F32)
            pT = [self.ps(st, "pT", [128, 8, 128], BF16) for _ in range(4)]
            ptk = [("pT", j) for j in range(4)]
            ctr = [0]
            deferred = []
            for b in range(ntok // TB):
                bb = b % 2
                S.dma("sp", lambda e, b=b, bb=bb: e.dma_start(out=ot[bb][:], in_=OT[b]),
                      reads=[("OT", b)], writes=[("ot", bb)], semkey=("ot", bb))
                for t in range(NT):
                    g = b * NT + t
                    p = g % 2
                    S.dma("sp", lambda e, g=g, p=p: e.dma_start(out=xf[p][:], in_=XFin[g * 128:(g + 1) * 128, :]),
                          reads=[("XF", g)], writes=[("xf", p)], semkey=("xf", p))
                    for n in range(4):
                        self.mmg(pY[:, n, :], [(ot[bb][:, k, t * 128:(t + 1) * 128], W[:, k, n * 512:(n + 1) * 512]) for k in range(KC)],
                                 reads=[("ot", bb)] + [("W", q) for q in range(4)], writes=[("pY", n)])
                        S.op("dve", lambda e, n=n, p=p: e.scalar_tensor_tensor(
                            out=hb[p][:, n * 512:(n + 1) * 512], in0=xf[p][:, n * 512:(n + 1) * 512], scalar=DN_ALPHA,
                            in1=pY[:, n, :], op0=ALU.mult, op1=ALU.add),
                            reads=[("xf", p), ("pY", n)], writes=[("hb", p)])
                    lp, lpb = self.ln_tile(g, hb[p], ("hb", p), L)
                    dst = final_out if final_out is not None else XFout
                    S.dma("pool", lambda e, g=g, lp=lp, dst=dst: e.dma_start(out=dst[g * 128:(g + 1) * 128, :], in_=L["xo"][lp][:]),
                          reads=[("xo", lp)], writes=[("XFo", g)], semkey=("xost", lp))
                    for fn_ in deferred:
                        fn_()
                    deferred = []
                    if XTout is not None:
                        def dtr(lp=lpb, bb=bb, t=t, b=b):
                            self.transpose_tile(lambda k, lp=lp: L["xob"][lp][:, k * 128:(k + 1) * 128], [("xob", lp)], blk[bb],
                                                t * 128, ("blk", bb), idb, pT, ptk, ctr)
                            if t == NT - 1:
                                S.dma("pool", lambda e, b=b, bb=bb: e.dma_start(out=XTout[b], in_=blk[bb][:]),
                                      reads=[("blk", bb)], writes=[("XTo", b)], semkey=("blkst", bb))
                        deferred.append(dtr)
            for fn_ in deferred:
                fn_()
            S.end_phase()

    def phase_memkv(self):
        S = self.S
        i, s = self.i, self.s
        NM = 3 * NMEM
        with contextlib.ExitStack() as st:
            c = self.load_consts(st, ["ident"])
            idb = c["idb"]
            xin = [self.sb(st, "xin", [128, D], F32) for _ in range(2)]
            xbf = [self.sb(st, "xbf", [128, D], BF16) for _ in range(2)]
            memT = self.sb(st, "memT", [128, KC, NM], BF16)
            wg = [self.sb(st, "wg", [128, KC, 512], BF16) for _ in range(2)]
            mk = self.sb(st, "mk", [128, KC, NM], BF16)
            mvv = self.sb(st, "mvv", [128, 6, D], BF16)
            pT = [self.ps(st, "pT", [128, 8, 128], BF16) for _ in range(2)]
            ptk = [("pT", j) for j in range(2)]
            pA = [self.ps(st, "pA", [128, 512], F32) for _ in range(2)]
            ctr = [0]
            for g in range(6):
                p = g % 2
                S.dma("sp", lambda e, g=g, p=p: e.dma_start(out=xin[p][:], in_=i["memA"][g * 128:(g + 1) * 128, :]),
                      writes=[("xin", p)], semkey=("xin", p))
                S.op("dve", lambda e, p=p: e.tensor_copy(out=xbf[p][:], in_=xin[p][:]), reads=[("xin", p)], writes=[("xbf", p)])
                self.transpose_tile(lambda k, p=p: xbf[p][:, k * 128:(k + 1) * 128], [("xbf", p)], memT, g * 128,
                                    "memT", idb, pT, ptk, ctr)
            ac = 0
            wc = 0
            for l in range(2):
                for gi in range(8):
                    wp = wc % 2
                    wc += 1
                    S.dma("sp", lambda e, l=l, gi=gi, wp=wp: e.dma_start(
                        out=wg[wp][:], in_=s["wb_mkv"][l][:, gi * 512:(gi + 1) * 512].rearrange("(k p) n -> p k n", p=128)),
                        reads=["wb_mkv"], writes=[("wg", wp)], semkey=("wg", wp))
                    if gi < 4:
                        for fc4 in range(4):
                            fc = gi * 4 + fc4
                            for (c0, cn) in ((0, 512), (512, NM - 512)):
                                a = ac % 2
                                ac += 1
                                self.mmg(pA[a][:, 0:cn], [(wg[wp][:, k, fc4 * 128:(fc4 + 1) * 128], memT[:, k, c0:c0 + cn]) for k in range(KC)],
                                         reads=[("wg", wp), "memT"], writes=[("pA", a)])
                                S.op("act", lambda e, a=a, fc=fc, c0=c0, cn=cn: e.copy(out=mk[:, fc, c0:c0 + cn], in_=pA[a][:, 0:cn]),
                                     reads=[("pA", a)], writes=["mk"])
                    else:
                        n = gi - 4
                        for mt in range(6):
                            a = ac % 2
                            ac += 1
                            self.mmg(pA[a][:, :], [(memT[:, k, mt * 128:(mt + 1) * 128], wg[wp][:, k, :]) for k in range(KC)],
                                     reads=[("wg", wp), "memT"], writes=[("pA", a)])
                            S.op("dve", lambda e, a=a, mt=mt, n=n: e.tensor_copy(out=mvv[:, mt, n * 512:(n + 1) * 512], in_=pA[a][:, :]),
                                 reads=[("pA", a)], writes=["mvv"])
                S.dma("sp", lambda e, l=l: e.dma_start(out=s["MK"][l], in_=mk[:]), reads=["mk"], writes=[("MK", l)], semkey="mkst")
                S.dma("sp", lambda e, l=l: e.dma_start(out=s["MV"][l], in_=mvv[:]), reads=["mvv"], writes=[("MV", l)], semkey="mvst")
            S.end_phase()

    def phase_xattn(self, layer, XT, OT, nseg, memidx):
        S = self.S
        s = self.s
        TB, NBS = self.TB, self.NBS
        sc = 512 ** -0.5
        with contextlib.ExitStack() as st:
            c = self.load_consts(st, ["ones"])
            ones = c["ones"]
            W = self.sb(st, "W", [128, KC, D], BF16)
            for q in range(4):
                S.dma("sp", lambda e, q=q: e.dma_start(out=W[:, q * 4:(q + 1) * 4, :],
                                                       in_=s["wb_mq"][layer][q * 512:(q + 1) * 512, :].rearrange("(k p) n -> p k n", p=128)),
                      reads=["wb_mq"], writes=[("W", q)], semkey=("W", q))
            mk = self.sb(st, "mk", [128, KC, NMEM], BF16)
            mvv = self.sb(st, "mvv", [128, 2, D], BF16)
            xt = [self.sb(st, "xt", [128, KC, TB], BF16) for _ in range(2)]
            qT = self.sb(st, "qT", [128, KC, TB], BF16)
            ob = [self.sb(st, "ob", [128, KC, TB], BF16) for _ in range(2)]
            PT = [self.sb(st, "PT", [128, 2, TB], BF16) for _ in range(2)]
            rden = [self.sb(st, "rden", [128, TB], F32) for _ in range(2)]
            pQ = [self.ps(st, "pQ", [128, 512], F32) for _ in range(2)]
            pS = [self.ps(st, "pS", [128, 512], F32) for _ in range(2)]
            pD = self.ps(st, "pD", [128, 512], F32)
            pO = [self.ps(st, "pO", [128, 512], F32) for _ in range(2)]
            qc = 0
            sc_ = 0
            oc = 0
            for sg in range(nseg):
                m = memidx[sg]
                S.dma("sp", lambda e, m=m: e.dma_start(out=mk[:], in_=s["MK"][layer][:, :, m * NMEM:(m + 1) * NMEM]),
                      reads=[("MK", layer)], writes=["mk"], semkey="mk")
                S.dma("sp", lambda e, m=m: e.dma_start(out=mvv[:], in_=s["MV"][layer][:, 2 * m:2 * m + 2, :]),
                      reads=[("MV", layer)], writes=["mvv"], semkey="mvv")
                for bs in range(NBS):
                    b = sg * NBS + bs
                    bb = b % 2
                    S.dma("sp", lambda e, b=b, bb=bb: e.dma_start(out=xt[bb][:], in_=XT[b]),
                          reads=[("XT", b)], writes=[("xt", bb)], semkey=("xt", bb))
                    for fc in range(KC):
                        a = qc % 2
                        qc += 1
                        self.mmg(pQ[a][:, 0:TB], [(W[:, k, fc * 128:(fc + 1) * 128], xt[bb][:, k, :]) for k in range(KC)],
                                 reads=[("xt", bb)] + [("W", q) for q in range(4)], writes=[("pQ", a)])
                        if fc % 2:
                            S.op("act", lambda e, a=a, fc=fc: e.copy(out=qT[:, fc, :], in_=pQ[a][:, 0:TB]),
                                 reads=[("pQ", a)], writes=[("qT", fc // 4)])
                        else:
                            S.op("dve", lambda e, a=a, fc=fc: e.tensor_copy(out=qT[:, fc, :], in_=pQ[a][:, 0:TB]),
                                 reads=[("pQ", a)], writes=[("qT", fc // 4)])
                    for h in range(4):
                        pp = (b * 4 + h) % 2
                        for kt in range(2):
                            a = sc_ % 2
                            sc_ += 1
                            self.mmg(pS[a][:, 0:TB], [(mk[:, h * 4 + c4, kt * 128:(kt + 1) * 128], qT[:, h * 4 + c4, :]) for c4 in range(4)],
                                     reads=["mk", ("qT", h)], writes=[("pS", a)])
                            S.op("act", lambda e, a=a, kt=kt, pp=pp: e.activation(out=PT[pp][:, kt, :], in_=pS[a][:, 0:TB], func=AF.Exp, scale=sc),
                                 reads=[("pS", a)], writes=[("PT", pp)])
                        self.mmg(pD[:, 0:TB], [(ones[:], PT[pp][:, kt, :]) for kt in range(2)],
                                 reads=["ones", ("PT", pp)], writes=["pD"])
                        S.op("dve", lambda e, pp=pp: e.reciprocal(out=rden[pp][:], in_=pD[:, 0:TB]),
                             reads=["pD"], writes=[("rden", pp)])
                        for c4 in range(4):
                            a = oc % 2
                            oc += 1
                            self.mmg(pO[a][:, 0:TB], [(mvv[:, kt, h * 512 + c4 * 128:h * 512 + (c4 + 1) * 128], PT[pp][:, kt, :]) for kt in range(2)],
                                     reads=["mvv", ("PT", pp)], writes=[("pO", a)])
                            S.op("dve", lambda e, a=a, h=h, c4=c4, bb=bb, pp=pp: e.tensor_tensor(
                                out=ob[bb][:, h * 4 + c4, :], in0=pO[a][:, 0:TB], in1=rden[pp][:], op=ALU.mult),
                                reads=[("pO", a), ("rden", pp)], writes=[("ob", bb)])
                    S.dma("pool", lambda e, b=b, bb=bb: e.dma_start(out=OT[b], in_=ob[bb][:]),
                          reads=[("ob", bb)], writes=[("OT", b)], semkey=("obst", bb))
            S.end_phase()

    def phase_mlp(self, layer, XT, XFin, XFout, XTout, ntok, final_out=None):
        S = self.S
        s = self.s
        TB, NT = self.TB, self.NT
        FG = 512
        NG = DFF // FG
        w1b = s["wb_w1"][layer]
        w2b = s["wb_w2"][layer]
        with contextlib.ExitStack() as st:
            c = self.load_consts(st, ["ident"])
            idb = c["idb"]
            L = self.ln_alloc(st, layer, 2, nxob=NT, nxo=1)
            xt = [self.sb(st, "xt", [128, KC, TB], BF16) for _ in range(2)]
            w1 = [self.sb(st, "w1", [128, KC, FG], BF16) for _ in range(2)]
            w2 = [self.sb(st, "w2", [128, 4, D], BF16) for _ in range(2)]
            hT = [self.sb(st, "hT", [128, 4, TB], BF16) for _ in range(2)]
            rl = [self.sb(st, "rl", [128, TB], F32) for _ in range(2)]
            ys = self.sb(st, "ys", [128, NT, D], F32)
            xf = [self.sb(st, "xf", [128, D], F32) for _ in range(1)]
            blk = [self.sb(st, "blk", [128, KC, TB], BF16) for _ in range(1)] if XTout is not None else None
            pH = [self.ps(st, "pH", [128, 512], F32) for _ in range(2)]
            pY = [self.ps(st, "pY", [128, 512], F32) for _ in range(4)]
            pT = [self.ps(st, "pT", [128, 8, 128], BF16) for _ in range(2)]
            ptk = [("pT", j) for j in range(2)]
            ctr = [0]
            hc = 0
            yc = 0
            wc = 0
            deferred = []
            nblk = ntok // TB
            pre = {}

            def load_x(b):
                if ("x", b) in pre or b >= nblk:
                    return
                pre[("x", b)] = 1
                bb_ = b % 2
                S.dma("sp", lambda e, b=b, bb_=bb_: e.dma_start(out=xt[bb_][:], in_=XT[b]),
                      reads=[("XT", b)], writes=[("xt", bb_)], semkey=("xt", bb_))

            def load_w(b, g):
                if b >= nblk:
                    return None
                if ("w", b, g) in pre:
                    return pre[("w", b, g)]
                wp = (b * NG + g) % 2
                pre[("w", b, g)] = wp
                S.dma("sp", lambda e, g=g, wp=wp: e.dma_start(
                    out=w1[wp][:], in_=w1b[:, g * FG:(g + 1) * FG].rearrange("(k p) n -> p k n", p=128)),
                    reads=["wb_w1"], writes=[("w1", wp)], semkey=("w1", wp))
                S.dma("sp", lambda e, g=g, wp=wp: e.dma_start(
                    out=w2[wp][:], in_=w2b[g * FG:(g + 1) * FG, :].rearrange("(k p) n -> p k n", p=128)),
                    reads=["wb_w2"], writes=[("w2", wp)], semkey=("w2", wp))
                return wp

            for b in range(nblk):
                bb = b % 2
                load_x(b)
                for g in range(NG):
                    wp = load_w(b, g)
                    if g == NG - 1:
                        load_x(b + 1)
                        load_w(b + 1, 0)
                    hp = g % 2
                    if g == 1:
                        for fn_ in deferred:
                            fn_()
                        deferred = []
                    for fc in range(4):
                        a = hc % 2
                        hc += 1
                        self.mmg(pH[a][:, 0:TB], [(w1[wp][:, k, fc * 128:(fc + 1) * 128], xt[bb][:, k, :]) for k in range(KC)],
                                 reads=[("w1", wp), ("xt", bb)], writes=[("pH", a)])
                        S.op("act", lambda e, a=a: e.activation(out=rl[a][:], in_=pH[a][:, 0:TB], func=AF.Relu),
                             reads=[("pH", a)], writes=[("rl", a)])
                        S.op("pool", lambda e, a=a, hp=hp, fc=fc: e.tensor_tensor(out=hT[hp][:, fc, :], in0=rl[a][:], in1=rl[a][:], op=ALU.mult),
                             reads=[("rl", a)], writes=[("hT", hp)])
                    for t in range(NT):
                        for n in range(4):
                            a = yc % 4
                            yc += 1
                            self.mmg(pY[a][:, :], [(hT[hp][:, fc, t * 128:(t + 1) * 128], w2[wp][:, fc, n * 512:(n + 1) * 512]) for fc in range(4)],
                                     reads=[("hT", hp), ("w2", wp)], writes=[("pY", a)])
                            if g == 0:
                                S.op("dve", lambda e, a=a, t=t, n=n: e.tensor_copy(out=ys[:, t, n * 512:(n + 1) * 512], in_=pY[a][:, :]),
                                     reads=[("pY", a)], writes=[("ys", t)])
                            else:
                                S.op("dve", lambda e, a=a, t=t, n=n: e.tensor_tensor(out=ys[:, t, n * 512:(n + 1) * 512],
                                                                                    in0=ys[:, t, n * 512:(n + 1) * 512], in1=pY[a][:, :], op=ALU.add),
                                     reads=[("pY", a), ("ys", t)], writes=[("ys", t)])
                for t in range(NT):
                    g_ = b * NT + t
                    p = 0
                    S.dma("sp", lambda e, g_=g_, p=p: e.dma_start(out=xf[p][:], in_=XFin[g_ * 128:(g_ + 1) * 128, :]),
                          reads=[("XF", g_)], writes=[("xf", p)], semkey=("xf", p))
                    S.op("dve", lambda e, t=t, p=p: e.scalar_tensor_tensor(out=ys[:, t, :], in0=xf[p][:], scalar=DN_ALPHA, in1=ys[:, t, :],
                                                                          op0=ALU.mult, op1=ALU.add),
                         reads=[("xf", p), ("ys", t)], writes=[("ys", t)])
                    lp, lpb = self.ln_tile(g_, ys[:, t, :], ("ys", t), L)
                    dst = final_out if final_out is not None else XFout
                    S.dma("sp", lambda e, g_=g_, lp=lp, dst=dst: e.dma_start(out=dst[g_ * 128:(g_ + 1) * 128, :], in_=L["xo"][lp][:]),
                          reads=[("xo", lp)], writes=[("XFo", g_)], semkey=("xost", lp))
                    if XTout is not None:
                        def dtr(lp=lpb, t=t, b=b):
                            self.transpose_tile(lambda k, lp=lp: L["xob"][lp][:, k * 128:(k + 1) * 128], [("xob", lp)], blk[0],
                                                t * 128, ("blk", 0), idb, pT, ptk, ctr)
                            if t == NT - 1:
                                S.dma("sp", lambda e, b=b: e.dma_start(out=XTout[b], in_=blk[0][:]),
                                      reads=[("blk", 0)], writes=[("XTo", b)], semkey=("blkst", 0))
                        deferred.append(dtr)
            for fn_ in deferred:
                fn_()
            S.end_phase()

    def phase_select(self, XFa, XTa, XF1, XT1):
        S = self.S
        SEG, TB, NBS = self.SEG, self.TB, self.NBS
        with contextlib.ExitStack() as st:
            om = self.sb(st, "om", [128, 4], F32)
            S.dma("sp", lambda e: e.dma_start(out=om[:], in_=self.i["ownmask"]), writes=["om"], semkey="om")
            cf = [self.sb(st, "cf", [128, D], F32) for _ in range(4)]
            af = [self.sb(st, "af", [128, D], F32) for _ in range(2)]
            cb = [self.sb(st, "cb", [128, KC * TB], BF16) for _ in range(4)]
            ab = [self.sb(st, "ab", [128, KC * TB], BF16) for _ in range(2)]
            TTs = SEG // 128
            S.dma("sp", lambda e: e.dma_start(out=XF1[0:2 * SEG, :], in_=XFa[0:2 * SEG, :]), reads=["XFall"], writes=["XF1"], semkey="cp1")
            S.dma("sp", lambda e: e.dma_start(out=XT1[0:2 * NBS], in_=XTa[0:2 * NBS]), reads=["XTall"], writes=["XT1"], semkey="cp2")
            for t in range(TTs):
                p = t % 2
                for j in range(4):
                    r0 = (2 + j) * SEG + t * 128
                    S.dma("sp", lambda e, j=j, r0=r0: e.dma_start(out=cf[j][:], in_=XFa[r0:r0 + 128, :]),
                          reads=["XFall"], writes=[("cf", j)], semkey=("cf", j))
                S.op("dve", lambda e, p=p: e.tensor_scalar(out=af[p][:], in0=cf[0][:], scalar1=om[:, 0:1], scalar2=None, op0=ALU.mult),
                     reads=[("cf", 0), "om"], writes=[("af", p)])
                for j in range(1, 4):
                    S.op("dve", lambda e, p=p, j=j: e.scalar_tensor_tensor(out=af[p][:], in0=cf[j][:], scalar=om[:, j:j + 1], in1=af[p][:],
                                                                          op0=ALU.mult, op1=ALU.add),
                         reads=[("cf", j), ("af", p), "om"], writes=[("af", p)])
                S.dma("sp", lambda e, t=t, p=p: e.dma_start(out=XF1[2 * SEG + t * 128:2 * SEG + (t + 1) * 128, :], in_=af[p][:]),
                      reads=[("af", p)], writes=["XF1"], semkey=("afst", p))
            for bs in range(NBS):
                p = bs % 2
                for j in range(4):
                    S.dma("sp", lambda e, j=j, bs=bs: e.dma_start(out=cb[j][:], in_=XTa[(2 + j) * NBS + bs].rearrange("p k t -> p (k t)")),
                          reads=["XTall"], writes=[("cb", j)], semkey=("cb", j))
                S.op("dve", lambda e, p=p: e.tensor_scalar(out=ab[p][:], in0=cb[0][:], scalar1=om[:, 0:1], scalar2=None, op0=ALU.mult),
                     reads=[("cb", 0), "om"], writes=[("ab", p)])
                for j in range(1, 4):
                    S.op("dve", lambda e, p=p, j=j: e.scalar_tensor_tensor(out=ab[p][:], in0=cb[j][:], scalar=om[:, j:j + 1], in1=ab[p][:],
                                                                          op0=ALU.mult, op1=ALU.add),
                         reads=[("cb", j), ("ab", p), "om"], writes=[("ab", p)])
                S.dma("sp", lambda e, bs=bs, p=p: e.dma_start(out=XT1[2 * NBS + bs].rearrange("p k t -> p (k t)"), in_=ab[p][:]),
                      reads=[("ab", p)], writes=["XT1"], semkey=("abst", p))
            S.end_phase()


def _bcast_mid(ap2d, n):
    a = ap2d.ap
    return bass.AP(ap2d.tensor, ap2d.offset, [list(a[0]), [0, n]] + [list(x) for x in a[1:]])


class Prog2(Prog):
    def phase_qkv(self, XT, nseg, do_q, cs_of_tile):
        S = self.S
        i, s = self.i, self.s
        SEG, TB, NT, NBS = self.SEG, self.TB, self.NT, self.NBS
        c0 = 0 if do_q else 2048
        ncol = 2048 if do_q else 1024
        ngr = ncol // 512
        with contextlib.ExitStack() as st:
            c = self.load_consts(st, ["ident"])
            idb = c["idb"]
            W = self.sb(st, "W", [128, KC, ncol], BF16)
            for q in range(4):
                S.dma("sp", lambda e, q=q: e.dma_start(out=W[:, q * 4:(q + 1) * 4, :],
                                                       in_=s["wb_qkv"][q * 512:(q + 1) * 512, c0:c0 + ncol].rearrange("(k p) n -> p k n", p=128)),
                      reads=["wb_qkv"], writes=[("W", q)], semkey=("W", q))
            gb = self.sb(st, "gb", [128, 128], F32)
            gsw = self.sb(st, "gsw", [128, 128], F32)
            gsrc = i["att_q_gain"] if do_q else i["att_k_gain"]
            S.dma("sp", lambda e: e.dma_start(out=gb[:], in_=gsrc.partition_broadcast(128)), writes=["gb"], semkey="gb")

            def mk_gsw(e):
                ins = None
                for a_ in range(2):
                    ins = e.tensor_copy(out=gsw[:, a_ * 64:a_ * 64 + 32], in_=gb[:, a_ * 64 + 32:a_ * 64 + 64])
                    ins = e.tensor_copy(out=gsw[:, a_ * 64 + 32:a_ * 64 + 64], in_=gb[:, a_ * 64:a_ * 64 + 32])
                return ins
            S.op("dve", mk_gsw, reads=["gb"], writes=["gsw"])
            epsc = self.sb(st, "epsc", [128, 1], F32)
            S.op("pool", lambda e: e.memset(epsc[:], EPS), writes=["epsc"])
            xt = [self.sb(st, "xt", [128, KC, TB], BF16) for _ in range(2)]
            cs = [self.sb(st, "cs", [128, 2, 128], F32) for _ in range(2)]
            cg = [self.sb(st, "cg", [128, 2, 128], F32) for _ in range(2)]
            sq = [self.sb(st, "sq", [128, 512], F32) for _ in range(2)]
            ssq = [self.sb(st, "ssq", [128, 4], F32) for _ in range(2)]
            xr = [self.sb(st, "xr", [128, 4, 128], F32) for _ in range(2)]
            t1 = [self.sb(st, "t1", [128, 4, 128], F32) for _ in range(2)]
            t2 = [self.sb(st, "t2", [128, 4, 128], F32) for _ in range(2)]
            xo = [self.sb(st, "xo", [128, 4, 128], BF16) for _ in range(3)]
            vb = [self.sb(st, "vb", [128, 512], BF16) for _ in range(2)]
            nch = 16 if do_q else 4
            blk = [self.sb(st, "blk", [128, nch, TB], BF16) for _ in range(2)]
            pG = [self.ps(st, "pG", [128, 512], F32) for _ in range(3)]
            pT = [self.ps(st, "pT", [128, 8, 128], BF16) for _ in range(2)]
            ptk = [("pT", j) for j in range(2)]
            ctr = [0]
            gc = 0
            xc = [0]
            deferred = []
            for sg in range(nseg):
                for bs in range(NBS):
                    b = sg * NBS + bs
                    bb = b % 2
                    S.dma("sp", lambda e, b=b, bb=bb: e.dma_start(out=xt[bb][:], in_=XT[b]),
                          reads=[("XT", b)], writes=[("xt", bb)], semkey=("xt", bb))
                    for t in range(NT):
                        g = b * NT + t
                        p = g % 2
                        csrc = cs_of_tile(sg, bs * NT + t)
                        S.dma("sp", lambda e, p=p, csrc=csrc: e.dma_start(out=cs[p][:], in_=csrc), writes=[("cs", p)], semkey=("cs", p))
                        S.op("pool", lambda e, p=p: e.tensor_tensor(out=cg[p][:, 0, :], in0=cs[p][:, 0, :], in1=gb[:], op=ALU.mult),
                             reads=[("cs", p), "gb"], writes=[("cg", p)])
                        S.op("pool", lambda e, p=p: e.tensor_tensor(out=cg[p][:, 1, :], in0=cs[p][:, 1, :], in1=gsw[:], op=ALU.mult),
                             reads=[("cs", p), "gsw"], writes=[("cg", p)])
                        for gr in range(ngr):
                            a = gc % 3
                            q2 = gc % 2
                            gc += 1
                            self.mmg(pG[a][:, :], [(xt[bb][:, k, t * 128:(t + 1) * 128], W[:, k, gr * 512:(gr + 1) * 512]) for k in range(KC)],
                                     reads=[("xt", bb)] + [("W", q) for q in range(4)], writes=[("pG", a)])
                            is_v = (not do_q) and gr == 1
                            if is_v:
                                S.op("act", lambda e, a=a, q2=q2: e.copy(out=vb[q2][:], in_=pG[a][:, :]), reads=[("pG", a)], writes=[("vb", q2)])
                                S.dma("sp", lambda e, g=g, q2=q2: e.dma_start(out=s["VV"][g * 128:(g + 1) * 128, :], in_=vb[q2][:]),
                                      reads=[("vb", q2)], writes=[("VV", g)], semkey=("vbst", q2))
                                continue
                            S.op("act", lambda e, a=a, q2=q2: e.activation(out=sq[q2][:], in_=pG[a][:, :], func=AF.Square),
                                 reads=[("pG", a)], writes=[("sq", q2)])
                            S.op("dve", lambda e, q2=q2: e.tensor_reduce(out=ssq[q2][:], in_=sq[q2][:].rearrange("p (h d) -> p h d", d=128),
                                                                         axis=AX.X, op=ALU.add),
                                 reads=[("sq", q2)], writes=[("ssq", q2)])
                            S.op("act", lambda e, q2=q2: e.activation(out=ssq[q2][:], in_=ssq[q2][:], func=AF.Ln, bias=epsc[:], scale=1.0 / 128),
                                 reads=[("ssq", q2), "epsc"], writes=[("ssq", q2)])
                            S.op("act", lambda e, q2=q2: e.activation(out=ssq[q2][:], in_=ssq[q2][:], func=AF.Exp, scale=-0.5),
                                 reads=[("ssq", q2)], writes=[("ssq", q2)])

                            def nrm(e, a=a, q2=q2):
                                ins = None
                                for h in range(4):
                                    ins = e.tensor_scalar(out=xr[q2][:, h, :], in0=pG[a][:, h * 128:(h + 1) * 128], scalar1=ssq[q2][:, h:h + 1],
                                                          scalar2=None, op0=ALU.mult)
                                return ins
                            S.op("dve", nrm, reads=[("pG", a), ("ssq", q2)], writes=[("xr", q2)])
                            S.op("pool", lambda e, q2=q2, p=p: e.tensor_tensor(out=t1[q2][:], in0=xr[q2][:], in1=_bcast_mid(cg[p][:, 0, :], 4), op=ALU.mult),
                                 reads=[("xr", q2), ("cg", p)], writes=[("t1", q2)])

                            def rot(e, q2=q2, p=p):
                                xv = xr[q2][:].rearrange("p h (a s d) -> p h a s d", a=2, s=2)
                                tv = t2[q2][:].rearrange("p h (a s d) -> p h a s d", a=2, s=2)
                                sv = cg[p][:, 1, :].rearrange("p (a s d) -> p a s d", a=2, s=2)
                                ins = None
                                for s_ in range(2):
                                    ins = e.tensor_tensor(out=tv[:, :, :, s_, :], in0=xv[:, :, :, 1 - s_, :],
                                                          in1=_bcast_mid(sv[:, :, s_, :], 4), op=ALU.mult)
                                return ins
                            S.op("pool", rot, reads=[("xr", q2), ("cg", p)], writes=[("t2", q2)])
                            xq = xc[0] % 3
                            xc[0] += 1
                            S.op("pool", lambda e, q2=q2, xq=xq: e.tensor_tensor(out=xo[xq][:], in0=t1[q2][:], in1=t2[q2][:], op=ALU.add),
                                 reads=[("t1", q2), ("t2", q2)], writes=[("xo", xq)])
                            while len(deferred) > 1:
                                deferred.pop(0)()
                            last_in_blk = (t == NT - 1) and (gr == (ngr - 1 if do_q else 0))

                            def dtr(q2=xq, bb=bb, t=t, gr=gr, b=b, last_in_blk=last_in_blk):
                                self.transpose_tile(lambda k, q2=q2: xo[q2][:, k, :], [("xo", q2)], blk[bb], t * 128, ("blk", bb), idb, pT, ptk, ctr,
                                                    nchunks=4, dst_c0=(gr * 4 if do_q else 0))
                                if last_in_blk:
                                    if do_q:
                                        S.dma("sp", lambda e, b=b, bb=bb: e.dma_start(out=s["QT"][b], in_=blk[bb][:]),
                                              reads=[("blk", bb)], writes=[("QT", b)], semkey=("blkst", bb))
                                    else:
                                        S.dma("sp", lambda e, b=b, bb=bb: e.dma_start(out=s["KT"][:, :, b * TB:(b + 1) * TB], in_=blk[bb][:]),
                                              reads=[("blk", bb)], writes=[("KT", b)], semkey=("blkst", bb))
                            deferred.append(dtr)
            for fn_ in deferred:
                fn_()
            S.end_phase()

    def phase_attn(self, OT):
        S = self.S
        s = self.s
        SEG, TB, NBS = self.SEG, self.TB, self.NBS
        sc = 128 ** -0.5
        Lmax = 4 * SEG
        with contextlib.ExitStack() as st:
            c = self.load_consts(st, ["ones"])
            ones = c["ones"]
            ksb = [self.sb(st, "ksb", [128, Lmax], BF16) for _ in range(2)]
            vsb = [self.sb(st, "vsb", [128, Lmax // 128, 128], BF16) for _ in range(2)]
            qt = [self.sb(st, "qt", [128, KC, TB], BF16) for _ in range(2)]
            ob = [self.sb(st, "ob", [128, KC, TB], BF16) for _ in range(2)]
            PT = [self.sb(st, "PT", [128, TB], BF16) for _ in range(4)]
            rden = [self.sb(st, "rden", [128, TB], F32) for _ in range(2)]
            pS = [self.ps(st, "pS", [128, 512], F32) for _ in range(4)]
            pO = [self.ps(st, "pO", [128, 512], F32) for _ in range(2)]
            pD = [self.ps(st, "pD", [128, 512], F32) for _ in range(2)]
            steps = []
            kc = 0
            hc = 0
            for sg in range(3):
                k0 = sg * SEG if sg < 2 else 2 * SEG
                L = SEG if sg < 2 else 4 * SEG
                nkt = L // 128
                for bs in range(NBS):
                    b = sg * NBS + bs
                    bb = b % 2
                    for g in range(4):
                        kp = kc % 2
                        kc += 1
                        for j in range(4):
                            hp = hc % 2
                            hc += 1
                            for kt in range(nkt):
                                steps.append(dict(b=b, bb=bb, g=g, kp=kp, j=j, hq=g * 4 + j, hp=hp, kt=kt, nkt=nkt, k0=k0, L=L,
                                                  first_b=(g == 0 and j == 0 and kt == 0), first_g=(j == 0 and kt == 0),
                                                  last_b=(g == 3 and j == 3 and kt == nkt - 1)))

            def emit_S(i):
                d = steps[i]
                if d["first_b"]:
                    S.dma("sp", lambda e, d=d: e.dma_start(out=qt[d["bb"]][:], in_=s["QT"][d["b"]]),
                          reads=[("QT", d["b"])], writes=[("qt", d["bb"])], semkey=("qt", d["bb"]))
                if d["first_g"]:
                    S.dma("sp", lambda e, d=d: e.dma_start(out=ksb[d["kp"]][:, 0:d["L"]], in_=s["KT"][:, d["g"], d["k0"]:d["k0"] + d["L"]]),
                          reads=["KTall"], writes=[("ksb", d["kp"])], semkey=("ksb", d["kp"]))
                    S.dma("sp", lambda e, d=d: e.dma_start(
                        out=vsb[d["kp"]][:, 0:d["nkt"], :],
                        in_=s["VV"][d["k0"]:d["k0"] + d["L"], d["g"] * 128:(d["g"] + 1) * 128].rearrange("(t p) d -> p t d", p=128)),
                        reads=["VVall"], writes=[("vsb", d["kp"])], semkey=("vsb", d["kp"]))
                a = i % 4
                self.mmg(pS[a][:, 0:TB], [(ksb[d["kp"]][:, d["kt"] * 128:(d["kt"] + 1) * 128], qt[d["bb"]][:, d["hq"], :])],
                         reads=[("ksb", d["kp"]), ("qt", d["bb"])], writes=[("pS", a)])

            LOOK = 2
            for i in range(min(LOOK, len(steps))):
                emit_S(i)
            for i, d in enumerate(steps):
                if i + LOOK < len(steps):
                    emit_S(i + LOOK)
                a = i % 4
                hp, kt, nkt, kp, bb, hq = d["hp"], d["kt"], d["nkt"], d["kp"], d["bb"], d["hq"]
                S.op("act", lambda e, a=a: e.activation(out=PT[a][:], in_=pS[a][:, 0:TB], func=AF.Exp, scale=sc),
                     reads=[("pS", a)], writes=[("PT", a)])
                self.mmg(pO[hp][:, 0:TB], [(vsb[kp][:, kt, :], PT[a][:])], reads=[("vsb", kp), ("PT", a)], writes=[("pO", hp)],
                         first=(kt == 0), last=(kt == nkt - 1))
                self.mmg(pD[hp][:, 0:TB], [(ones[:], PT[a][:])], reads=["ones", ("PT", a)], writes=[("pD", hp)],
                         first=(kt == 0), last=(kt == nkt - 1))
                if kt == nkt - 1:
                    S.op("dve", lambda e, hp=hp: e.reciprocal(out=rden[hp][:], in_=pD[hp][:, 0:TB]),
                         reads=[("pD", hp)], writes=[("rden", hp)])
                    S.op("dve", lambda e, hp=hp, hq=hq, bb=bb: e.tensor_tensor(out=ob[bb][:, hq, :], in0=pO[hp][:, 0:TB], in1=rden[hp][:], op=ALU.mult),
                         reads=[("pO", hp), ("rden", hp)], writes=[("ob", bb)])
                if d["last_b"]:
                    S.dma("pool", lambda e, d=d: e.dma_start(out=OT[d["b"]], in_=ob[d["bb"]][:]),
                          reads=[("ob", d["bb"])], writes=[("OT", d["b"])], semkey=("obst", d["bb"]))
            S.end_phase()


class Prog3(Prog2):
    def phase_gla(self, XT, OT, seqs):
        S = self.S
        i, s = self.i, self.s
        TB, NT = self.TB, self.NT
        wg = s["wb_gin"]
        with contextlib.ExitStack() as st:
            c = self.load_consts(st, ["ident"])
            idb = c["idb"]
            cn = {}
            for nm, shp in (("trif", [128, 130]), ("trib", [128, 130]), ("trikf", [128, 128]), ("trikb", [128, 128]),
                            ("maskf", [128, 128]), ("maskb", [128, 128])):
                cn[nm] = self.sb(st, nm, shp, F32)
                S.dma("sp", lambda e, nm=nm: e.dma_start(out=cn[nm][:], in_=i[nm]), writes=[nm], semkey=nm)
            m2 = self.sb(st, "m2", [128, 2, 128], F32)
            S.dma("sp", lambda e: e.dma_start(out=m2[:, 0, :], in_=i["maskf"]), writes=["m2"], semkey="m2a")
            S.dma("sp", lambda e: e.dma_start(out=m2[:, 1, :], in_=i["maskb"]), writes=["m2"], semkey="m2b")
            wup = self.sb(st, "wup", [17, 2, 1024], BF16)
            for d_ in range(2):
                S.dma("pool", lambda e, d_=d_: e.dma_start(out=wup[0:16, d_, :], in_=i["gla_w_gate_up"][d_]), writes=["wup"], semkey=("wup", d_))
                S.dma("pool", lambda e, d_=d_: e.dma_start(out=wup[16:17, d_, :], in_=i["gla_b_gate"][d_:d_ + 1, :]), writes=["wup"], semkey=("wupb", d_))
            ngb = self.sb(st, "ngb", [128, 512], F32)
            S.dma("sp", lambda e: e.dma_start(out=ngb[:], in_=i["gla_norm_g"].partition_broadcast(128)), writes=["ngb"], semkey="ngb")
            epsc = self.sb(st, "epsc", [128, 1], F32)
            S.op("pool", lambda e: e.memset(epsc[:], EPS), writes=["epsc"])
            wqk = self.sb(st, "wqk", [128, KC, 512], BF16)
            wv = self.sb(st, "wv", [128, KC, 512], BF16)
            wr = self.sb(st, "wr", [128, KC, 512], BF16)
            wz = self.sb(st, "wz", [128, KC, 32], BF16)
            S.dma("sp", lambda e: e.dma_start(out=wz[:], in_=wg[:, 6144:6176].rearrange("(k p) n -> p k n", p=128)),
                  reads=["wb_gin"], writes=["wz"], semkey="wz")
            xt = [self.sb(st, "xt", [128, KC, TB], BF16) for _ in range(2)]
            zT = [self.sb(st, "zT", [17, 2, TB], BF16) for _ in range(2)]
            for z_ in range(2):
                S.op("pool", lambda e, z_=z_: e.memset(zT[z_][:], 1.0), writes=[("zT", z_)])
            e1 = self.sb(st, "e1", [128, 2, 256], F32)
            sp = [self.sb(st, "sp", [128, 2, 256], F32) for _ in range(3)]
            EK = self.sb(st, "EK", [128, 2, 256], F32)
            ktmp = [self.sb(st, "ktmp", [128, 256], BF16) for _ in range(2)]
            ET = self.sb(st, "ET", [128, 2, 2, 128], F32)
            EiT = self.sb(st, "EiT", [128, 2, 2, 128], F32)
            dec = [self.sb(st, "dec", [128, 4], F32) for _ in range(3)]
            qT = [self.sb(st, "qT", [128, 2, TB], F32) for _ in range(2)]
            kT = [self.sb(st, "kT", [128, 2, TB], F32) for _ in range(2)]
            qe = [self.sb(st, "qe", [128, 2, 2, 128], BF16) for _ in range(3)]
            ke = [self.sb(st, "ke", [128, 2, 2, 128], BF16) for _ in range(3)]
            kv = [self.sb(st, "kv", [128, 768], BF16) for _ in range(3)]
            kd = [self.sb(st, "kd", [128, 256], BF16) for _ in range(3)]
            rs = [self.sb(st, "rs", [128, 512], F32) for _ in range(3 * NT)]
            Pm = [self.sb(st, "Pm", [128, 2, 128], BF16) for _ in range(2)]
            S32 = self.sb(st, "S32", [128, 2, 512], F32)
            Sb = [self.sb(st, "Sb", [128, 2, 512], BF16) for _ in range(4)]
            SbL = [self.sb(st, "SbL", [128, 2, 2, 512], BF16) for _ in range(3)]
            junk = self.sb(st, "junk", [128, 512], F32)
            ssq = self.sb(st, "ssq", [128, 1], F32)
            on1 = self.sb(st, "on1", [128, 512], F32)
            on2 = [self.sb(st, "on2", [128, 512], BF16) for _ in range(2)]
            oblk = [self.sb(st, "oblk", [128, 4, TB], BF16) for _ in range(2)]
            pg = self.ps(st, "pg", [128, 512], F32)
            pc = self.ps(st, "pc", [128, 2, 2, 128], F32)
            pka = self.ps(st, "pka", [128, 512], F32)
            po = self.ps(st, "po", [128, 512], F32)
            pu = [self.ps(st, "pu", [128, 512], F32) for _ in range(2)]
            pP = [self.ps(st, "pP", [128, 512], F32) for _ in range(2)]
            pTt = [pP[0][:].bitcast(BF16), pP[1][:].bitcast(BF16)]
            tri = {0: cn["trif"], 1: cn["trib"]}
            trik = {0: cn["trikf"], 1: cn["trikb"]}
            KVst = s["KVst"]
            st_ = {"pp": 0, "sb": 0}

            def nextpp():
                a = st_["pp"] % 2
                st_["pp"] += 1
                return a

            def s_cast():
                q = st_["sb"] % 4
                st_["sb"] += 1
                S.op("act", lambda e, q=q: e.copy(out=Sb[q][:], in_=S32[:]), reads=["S32"], writes=[("Sb", q)])
                return q

            def run_skewed(n, stages):
                maxlead = max(l for l, _ in stages)
                minlead = min(l for l, _ in stages)
                for it in range(-maxlead, n - minlead):
                    for lead, fn in stages:
                        g = it + lead
                        if 0 <= g < n:
                            fn(g)

            for (b0, nb) in seqs:
                ntile = nb * NT
                for h in range(4):
                    S.dma("sp", lambda e, h=h: e.dma_start(out=wqk[:, :, 0:256], in_=wg[:, h * 256:(h + 1) * 256].rearrange("(k p) n -> p k n", p=128)),
                          reads=["wb_gin"], writes=["wqk"], semkey="wqk")
                    S.dma("sp", lambda e, h=h: e.dma_start(out=wqk[:, :, 256:512], in_=wg[:, 1024 + h * 256:1024 + (h + 1) * 256].rearrange("(k p) n -> p k n", p=128)),
                          reads=["wb_gin"], writes=["wqk"], semkey="wqk2")
                    S.dma("sp", lambda e, h=h: e.dma_start(out=wv[:], in_=wg[:, 2048 + h * 512:2048 + (h + 1) * 512].rearrange("(k p) n -> p k n", p=128)),
                          reads=["wb_gin"], writes=["wv"], semkey="wv")
                    S.dma("sp", lambda e, h=h: e.dma_start(out=wr[:], in_=wg[:, 4096 + h * 512:4096 + (h + 1) * 512].rearrange("(k p) n -> p k n", p=128)),
                          reads=["wb_gin"], writes=["wr"], semkey="wr")

                    S.op("pool", lambda e: e.memset(S32[:], 0.0), writes=["S32"])

                    def btile(j):
                        g = ntile - 1 - j
                        return g // NT, g % NT, g

                    def B_blk(j, dirs):
                        bl, t, g = btile(j) if dirs == [1, 0] else (j // NT, j % NT, j)
                        first = (t == NT - 1) if dirs == [1, 0] else (t == 0)
                        if not first:
                            return
                        bb = bl % 2
                        S.dma("sp", lambda e, bl=bl, bb=bb, b0=b0: e.dma_start(out=xt[bb][:], in_=XT[b0 + bl]), reads=[("XT", b0 + bl)], writes=[("xt", bb)], semkey=("xt", bb))
                        for d_ in dirs:
                            a = nextpp()
                            self.mmg(pP[a][0:16, 0:TB], [(wz[:, k, d_ * 16:(d_ + 1) * 16], xt[bb][:, k, :]) for k in range(KC)],
                                     reads=["wz", ("xt", bb)], writes=[("pP", a)])
                            S.op("act", lambda e, d_=d_, a=a, bb=bb: e.copy(out=zT[bb][0:16, d_, :], in_=pP[a][0:16, 0:TB]), reads=[("pP", a)], writes=[("zT", bb)])

                    def gates(bl, t, g, dirs):
                        bb = bl % 2
                        s3 = g % 3
                        for d_ in dirs:
                            self.mmg(pg[:, d_ * 256:(d_ + 1) * 256], [(zT[bb][:, d_, t * 128:(t + 1) * 128], wup[:, d_, h * 256:(h + 1) * 256])],
                                     reads=[("zT", bb), "wup"], writes=["pg"])
                        lo, hi = dirs[0], dirs[-1] + 1
                        S.op("act", lambda e: e.activation(out=e1[:, lo:hi, :], in_=pg[:, lo * 256:hi * 256].rearrange("p (a b) -> p a b", b=256), func=AF.Exp, scale=-1.0),
                             reads=["pg"], writes=["e1"])
                        S.op("act", lambda e, s3=s3: e.activation(out=sp[s3][:, lo:hi, :], in_=e1[:, lo:hi, :], func=AF.Ln, bias=1.0, scale=1.0),
                             reads=["e1"], writes=[("sp", s3)])

                    def bT1(j):
                        B_blk(j, [1, 0])
                        bl, t, g = btile(j)
                        gates(bl, t, g, [0, 1])

                    def bT2(j):
                        bl, t, g = btile(j)
                        bb = bl % 2
                        s3 = g % 3
                        for d_ in range(2):
                            self.mmg(pka[:, d_ * 256:(d_ + 1) * 256], [(trik[d_][:], sp[s3][:, d_, :])], reads=["trikf", "trikb", ("sp", s3)], writes=["pka"])
                        S.op("act", lambda e: e.activation(out=EK[:].rearrange("p a b -> p (a b)"), in_=pka[:, 0:512], func=AF.Exp), reads=["pka"], writes=["EK"])
                        a = nextpp()
                        kq = g % 2
                        self.mmg(pP[a][:, 0:256], [(xt[bb][:, k, t * 128:(t + 1) * 128], wqk[:, k, 256:512]) for k in range(KC)],
                                 reads=[("xt", bb), "wqk"], writes=[("pP", a)])
                        S.op("act", lambda e, a=a, kq=kq: e.copy(out=ktmp[kq][:], in_=pP[a][:, 0:256]), reads=[("pP", a)], writes=[("ktmp", kq)])
                        S.op("dve", lambda e, s3=s3, kq=kq: e.tensor_tensor(out=kd[s3][:], in0=ktmp[kq][:], in1=EK[:, 1, :], op=ALU.mult),
                             reads=[("ktmp", kq), "EK"], writes=[("kd", s3)])
                        S.op("dve", lambda e, s3=s3, kq=kq: e.tensor_tensor(out=kv[s3][:, 0:256], in0=ktmp[kq][:], in1=EK[:, 0, :], op=ALU.mult),
                             reads=[("ktmp", kq), "EK"], writes=[("kvk", s3)])
                        a2 = nextpp()
                        self.mmg(pP[a2][:, :], [(xt[bb][:, k, t * 128:(t + 1) * 128], wv[:, k, :]) for k in range(KC)],
                                 reads=[("xt", bb), "wv"], writes=[("pP", a2)])
                        S.op("act", lambda e, a2=a2, s3=s3: e.copy(out=kv[s3][:, 256:768], in_=pP[a2][:, :]), reads=[("pP", a2)], writes=[("kvv", s3)])
                        S.dma("sp", lambda e, g=g, s3=s3: e.dma_start(out=KVst[g], in_=kv[s3][:]), reads=[("kvk", s3), ("kvv", s3)], writes=[("KVst", g)],
                              semkey=("kvst", s3))
                        pcf = pc[:].rearrange("p a b c -> p (a b c)")
                        for dk in range(2):
                            self.mmg(pcf[:, dk * 2:dk * 2 + 2], [(sp[s3][:, 1, dk * 128:(dk + 1) * 128], tri[1][:, 128:130])],
                                     reads=[("sp", s3), "trib"], writes=["pc"])
                        S.op("act", lambda e, s3=s3, pcf=pcf: e.activation(out=dec[s3][:], in_=pcf[:, 0:4], func=AF.Exp), reads=["pc"], writes=[("dec", s3)])

                    def bT3(j):
                        bl, t, g = btile(j)
                        s3 = g % 3
                        for ci in (1, 0):
                            cidx = g * 2 + ci
                            q = s_cast()
                            S.dma("sp", lambda e, q=q, cidx=cidx: e.dma_start(out=s["SBst"][cidx], in_=Sb[q][:]),
                                  reads=[("Sb", q)], writes=[("SBst", cidx)], semkey=("sbst", q))
                            for dk in range(2):
                                self.mmg(pu[dk][:, :], [(kd[s3][ci * 64:(ci + 1) * 64, dk * 128:(dk + 1) * 128], kv[s3][ci * 64:(ci + 1) * 64, 256:768])],
                                         reads=[("kd", s3), ("kvv", s3)], writes=[("pu", dk)])
                            for dk in range(2):
                                S.op("dve", lambda e, dk=dk, ci=ci, s3=s3: e.scalar_tensor_tensor(
                                    out=S32[:, dk, :], in0=S32[:, dk, :], scalar=dec[s3][:, dk * 2 + ci:dk * 2 + ci + 1],
                                    in1=pu[dk][:, :], op0=ALU.mult, op1=ALU.add),
                                    reads=[("pu", dk), ("dec", s3), "S32"], writes=["S32"])

                    import os as _os
                    run_skewed(ntile, [(2, bT1), (1, bT2), (0, bT3)][:int(_os.environ.get("GLA_B", "3"))])

                    S.op("pool", lambda e: e.memset(S32[:], 0.0), writes=["S32"])
                    fst = {"q": s_cast()}

                    def fB_tasks(bl):
                        bb = bl % 2
                        tasks = []

                        def t_load():
                            S.dma("sp", lambda e, bl=bl, bb=bb, b0=b0: e.dma_start(out=xt[bb][:], in_=XT[b0 + bl]), reads=[("XT", b0 + bl)], writes=[("xt", bb)], semkey=("xt", bb))
                            for d_ in (0, 1):
                                a = nextpp()
                                self.mmg(pP[a][0:16, 0:TB], [(wz[:, k, d_ * 16:(d_ + 1) * 16], xt[bb][:, k, :]) for k in range(KC)],
                                         reads=["wz", ("xt", bb)], writes=[("pP", a)])
                                S.op("act", lambda e, d_=d_, a=a, bb=bb: e.copy(out=zT[bb][0:16, d_, :], in_=pP[a][0:16, 0:TB]), reads=[("pP", a)], writes=[("zT", bb)])
                        tasks.append(t_load)
                        for (dst, key, c0) in ((qT, "qT", 0), (kT, "kT", 256)):
                            for dk in range(2):
                                def t_qk(dst=dst, key=key, c0=c0, dk=dk):
                                    a = nextpp()
                                    self.mmg(pP[a][:, 0:TB], [(wqk[:, k, c0 + dk * 128:c0 + (dk + 1) * 128], xt[bb][:, k, :]) for k in range(KC)],
                                             reads=["wqk", ("xt", bb)], writes=[("pP", a)])
                                    S.op("act", lambda e, a=a, dst=dst, dk=dk, bb=bb: e.copy(out=dst[bb][:, dk, :], in_=pP[a][:, 0:TB]),
                                         reads=[("pP", a)], writes=[(key, bb)])
                                tasks.append(t_qk)
                        for t2 in range(NT):
                            def t_r(t2=t2):
                                a = nextpp()
                                ri = (bl % 3) * NT + t2
                                self.mmg(pP[a][:, :], [(xt[bb][:, k, t2 * 128:(t2 + 1) * 128], wr[:, k, :]) for k in range(KC)],
                                         reads=[("xt", bb), "wr"], writes=[("pP", a)])
                                S.op("act", lambda e, a=a, ri=ri: e.activation(out=rs[ri][:], in_=pP[a][:, :], func=AF.Silu),
                                     reads=[("pP", a)], writes=[("rs", ri)])
                            tasks.append(t_r)
                        return tasks

                    nblk_f = ntile // NT
                    spread = (NT == 4 and nblk_f > 1)
                    sched_tasks = {}
                    if spread:
                        for bl in range(1, nblk_f):
                            tk = fB_tasks(bl)
                            g0 = (bl - 1) * NT
                            sched_tasks[g0] = [tk[0], tk[1]]
                            sched_tasks[g0 + 1] = [tk[2], tk[3]]
                            sched_tasks[g0 + 2] = [tk[4], tk[5]]
                            sched_tasks[g0 + 3] = [tk[6], tk[7], tk[8]]

                    def fB(g):
                        bl, t = g // NT, g % NT
                        if t != 0:
                            return
                        if spread and bl > 0:
                            return
                        for fn_ in fB_tasks(bl):
                            fn_()

                    def fBs(g):
                        for fn_ in sched_tasks.get(g, ()):
                            fn_()

                    def fS1(g):
                        fB(g)
                        gates(g // NT, g % NT, g, [0, 1])

                    def fS2(g):
                        bl, t = g // NT, g % NT
                        bb = bl % 2
                        s3 = g % 3
                        tsl = slice(t * 128, (t + 1) * 128)
                        S.dma("sp", lambda e, g=g, s3=s3: e.dma_start(out=kv[s3][:], in_=KVst[g]), reads=[("KVst", g)], writes=[("kvk", s3), ("kvv", s3)],
                              semkey=("kvld", s3))
                        S.dma("sp", lambda e, s3=s3, g=g: e.dma_start(out=SbL[s3][:], in_=s["SBst"][g * 2:g * 2 + 2].rearrange("c p a b -> p c a b")),
                              reads=[("SBst", g * 2), ("SBst", g * 2 + 1)], writes=[("SbL", s3)], semkey=("sbl", s3))
                        for d_ in range(2):
                            for dk in range(2):
                                self.mmg(pc[:, d_, dk, :], [(sp[s3][:, d_, dk * 128:(dk + 1) * 128], tri[d_][:, 0:128])],
                                         reads=[("sp", s3), "trif", "trib"], writes=["pc"])
                        S.op("act", lambda e: e.activation(out=ET[:].rearrange("p a b c -> p (a b c)"), in_=pc[:].rearrange("p a b c -> p (a b c)"), func=AF.Exp),
                             reads=["pc"], writes=["ET"])
                        S.op("act", lambda e: e.activation(out=EiT[:].rearrange("p a b c -> p (a b c)"), in_=pc[:].rearrange("p a b c -> p (a b c)"), func=AF.Exp, scale=-1.0),
                             reads=["pc"], writes=["EiT"])

                        def mkdec(e, s3=s3):
                            ins = None
                            for dk in range(2):
                                for ci in range(2):
                                    ins = e.tensor_copy(out=dec[s3][:, dk * 2 + ci:dk * 2 + ci + 1], in_=ET[:, 0, dk, ci * 64 + 63:ci * 64 + 64])
                            return ins
                        S.op("dve", mkdec, reads=["ET"], writes=[("dec", s3)])
                        for d_ in range(2):
                            S.op("dve", lambda e, d_=d_, s3=s3, tsl=tsl, bb=bb: e.scalar_tensor_tensor(
                                out=qe[s3][:, d_, :, :], in0=qT[bb][:, :, tsl], scalar=1.0 / 16, in1=ET[:, d_, :, :], op0=ALU.mult, op1=ALU.mult),
                                reads=[("qT", bb), "ET"], writes=[("qe", s3)])
                            S.op("pool", lambda e, d_=d_, s3=s3, tsl=tsl, bb=bb: e.tensor_tensor(out=ke[s3][:, d_, :, :], in0=kT[bb][:, :, tsl], in1=EiT[:, d_, :, :], op=ALU.mult),
                                 reads=[("kT", bb), "EiT"], writes=[("ke", s3)])

                    def fS3(g):
                        s3 = g % 3
                        p2 = g % 2
                        for d_ in range(2):
                            self.mmg(pka[:, d_ * 128:(d_ + 1) * 128], [(ke[s3][:, d_, dk, :], qe[s3][:, d_, dk, :]) for dk in range(2)],
                                     reads=[("ke", s3), ("qe", s3)], writes=["pka"])
                        S.op("dve", lambda e, p2=p2: e.tensor_tensor(out=Pm[p2][:].rearrange("p a b -> p (a b)"), in0=pka[:, 0:256],
                                                                     in1=m2[:].rearrange("p a b -> p (a b)"), op=ALU.mult),
                             reads=["pka", "m2"], writes=[("Pm", p2)])

                    def s_upd(s3, ci):
                        for dk in range(2):
                            S.op("dve", lambda e, dk=dk, ci=ci, s3=s3: e.scalar_tensor_tensor(
                                out=S32[:, dk, :], in0=S32[:, dk, :], scalar=dec[s3][:, dk * 2 + ci:dk * 2 + ci + 1],
                                in1=pu[dk][:, :], op0=ALU.mult, op1=ALU.add),
                                reads=[("pu", dk), ("dec", s3), "S32"], writes=["S32"])

                    def u_mm(s3, ci):
                        for dk in range(2):
                            self.mmg(pu[dk][:, :], [(kv[s3][ci * 64:(ci + 1) * 64, dk * 128:(dk + 1) * 128], kv[s3][ci * 64:(ci + 1) * 64, 256:768])],
                                     reads=[("kvk", s3), ("kvv", s3)], writes=[("pu", dk)])

                    def fS4a(g):
                        s3 = g % 3
                        p2 = g % 2
                        q0 = fst["q"]
                        u_mm(s3, 0)
                        self.mmg(po[:, :], [(Pm[p2][:, 0, :], kv[s3][:, 256:768]), (Pm[p2][:, 1, :], kv[s3][:, 256:768])],
                                 reads=[("Pm", p2), ("kvv", s3)], writes=["po"], first=True, last=False)
                        for ci in range(2):
                            csl = slice(ci * 64, (ci + 1) * 64)
                            self.mmg(po[csl, :], [(qe[s3][:, 1, dk, csl], SbL[s3][:, ci, dk, :]) for dk in range(2)],
                                     reads=[("qe", s3), ("SbL", s3)], writes=["po"], first=False, last=False)
                        self.mmg(po[0:64, :], [(qe[s3][:, 0, dk, 0:64], Sb[q0][:, dk, :]) for dk in range(2)],
                                 reads=[("qe", s3), ("Sb", q0)], writes=["po"], first=False, last=False)
                        s_upd(s3, 0)
                        fst["q1"] = s_cast()

                    def fS4b(g):
                        s3 = g % 3
                        p2 = g % 2
                        q1 = fst["q1"]
                        u_mm(s3, 1)
                        self.mmg(po[64:128, :], [(qe[s3][:, 0, dk, 64:128], Sb[q1][:, dk, :]) for dk in range(2)],
                                 reads=[("qe", s3), ("Sb", q1)], writes=["po"], first=False, last=True)
                        s_upd(s3, 1)
                        fst["q"] = s_cast()
                        bl, t = g // NT, g % NT
                        ri = (bl % 3) * NT + t
                        S.op("act", lambda e: e.activation(out=junk[:], in_=po[:, :], func=AF.Square, accum_out=ssq[:]),
                             reads=["po"], writes=["junk", "ssq"])
                        S.op("act", lambda e: e.activation(out=ssq[:], in_=ssq[:], func=AF.Ln, bias=epsc[:], scale=1.0 / 512),
                             reads=["ssq", "epsc"], writes=["ssq"])
                        S.op("act", lambda e: e.activation(out=ssq[:], in_=ssq[:], func=AF.Exp, scale=-0.5), reads=["ssq"], writes=["ssq"])
                        S.op("dve", lambda e: e.scalar_tensor_tensor(out=on1[:], in0=po[:, :], scalar=ssq[:], in1=ngb[:], op0=ALU.mult, op1=ALU.mult),
                             reads=["po", "ssq", "ngb"], writes=["on1"])
                        S.op("pool", lambda e, p2=p2, ri=ri: e.tensor_tensor(out=on2[p2][:], in0=on1[:], in1=rs[ri][:], op=ALU.mult),
                             reads=["on1", ("rs", ri)], writes=[("on2", p2)])

                    def fS5(g):
                        bl, t = g // NT, g % NT
                        p2 = g % 2
                        ob_ = bl % 2
                        a = nextpp()

                        def tp(e, a=a, p2=p2):
                            ins = None
                            for k in range(4):
                                ins = e.transpose(out=pTt[a][:, k * 128:(k + 1) * 128], in_=on2[p2][:, k * 128:(k + 1) * 128], identity=idb[:])
                            return ins
                        S.op("pe", tp, reads=[("on2", p2), "idb"], writes=[("pP", a)])
                        S.op("act", lambda e, a=a, ob_=ob_, t=t: e.copy(out=oblk[ob_][:, :, t * 128:(t + 1) * 128],
                                                                        in_=pTt[a][:, 0:512].rearrange("p (k t) -> p k t", t=128)),
                             reads=[("pP", a)], writes=[("oblk", ob_)])
                        if t == NT - 1:
                            b = b0 + bl
                            S.dma("sp", lambda e, b=b, ob_=ob_, h=h: e.dma_start(out=OT[b][:, h * 4:(h + 1) * 4, :], in_=oblk[ob_][:]),
                                  reads=[("oblk", ob_)], writes=[("OT", b)], semkey=("oblkst", ob_))

                    import os as _os
                    _m = _os.environ.get("GLA_DBG", "")
                    if _m == "bwd":
                        continue
                    _stg = [(0, fBs), (0, fS4a), (3, fS1), (1, fS3), (0, fS4b), (2, fS2), (-1, fS5)]
                    if _m.startswith("n"):
                        _stg = _stg[:int(_m[1:])]
                    run_skewed(ntile, _stg)
            S.end_phase()

    def build(self, upto=99):
        SEG, NBS = self.SEG, self.NBS
        NA = self.NSA * SEG
        NB_ = self.NSB * SEG
        with contextlib.ExitStack() as gst:
            self.declare()
            self.S = Sched(self.nc, gst)
            i, s = self.i, self.s
            self.phase_weights(0)
            self.phase_prep(i["xA"], s["XTa"], NA)
            self.phase_weights(1)
            self.phase_memkv()
            seqs = [(0, NBS), (NBS, NBS), (2 * NBS, 4 * NBS)]
            memA = [0, 1, 2, 2, 2, 2]
            memB = [0, 1, 2]
            self.phase_gla(s["XTa"], s["OT"], seqs)
            if upto == -1:
                S = self.S
                dbg2 = self.dout("dbgOT", [NA // self.TB, 128, KC, self.TB], BF16)
                S.dma("sp", lambda e: e.dma_start(out=dbg2, in_=s["OT"]), semkey="dump")
                S.dma("sp", lambda e: e.dma_start(out=self.y[0:128, :], in_=i["xA"][0:128, :]), semkey="dump2")
                S.end_phase()
                return
            self.phase_proj_ln(s["wb_gout"], "wb_gout", s["OT"], i["xA"], s["XFa"], s["XTb"], NA, 0, 0)
            self.phase_xattn(0, s["XTb"], s["OT"], 6, memA)
            self.phase_proj_ln(s["wb_mo"][0], "wb_mo", s["OT"], s["XFa"], s["XFb"], s["XTa"], NA, 0, 1)
            self.phase_mlp(0, s["XTa"], s["XFb"], s["XFa"], s["XTb"], NA)
            if upto == 0:
                self._dump(s["XFa"], NA)
                return
            csA = i["csA"]
            self.phase_qkv(s["XTb"], 6, False, lambda sg, tt: csA[(0 if sg < 2 else (sg - 2) * SEG) + tt * 128:(0 if sg < 2 else (sg - 2) * SEG) + (tt + 1) * 128])
            self.phase_select(s["XFa"], s["XTb"], s["XF1"], s["XT1"])
            self.phase_qkv(s["XT1"], 3, True, lambda sg, tt: (csA if sg < 2 else i["csOwn"])[tt * 128:(tt + 1) * 128])
            self.phase_attn(s["OT"])
            self.phase_proj_ln(s["wb_aout"], "wb_aout", s["OT"], s["XF1"], s["XFb"], s["XTa"], NB_, 1, 0)
            self.phase_xattn(1, s["XTa"], s["OT"], 3, memB)
            self.phase_proj_ln(s["wb_mo"][1], "wb_mo", s["OT"], s["XFb"], s["XFa"], s["XT1"], NB_, 1, 1)
            self.phase_mlp(1, s["XT1"], s["XFa"], None, None, NB_, final_out=self.y)

    def _dump(self, src, n):
        S = self.S
        S.dma("sp", lambda e: e.dma_start(out=self.dbgout[0:n, :], in_=src[0:n, :]), semkey="dump")
        S.dma("sp", lambda e: e.dma_start(out=self.y[:, :], in_=src[0:self.NSB * self.SEG, :]), semkey="dump2")
        S.end_phase()


def _consts(SEG):
    j = np.arange(128)[:, None]
    t = np.arange(128)[None, :]
    same = (j // 64) == (t // 64)
    g = -1.0 / 16.0
    c = {}
    trif = np.zeros((128, 130), np.float32)
    trib = np.zeros((128, 130), np.float32)
    trif[:, :128] = g * (same & (j <= t))
    trib[:, :128] = g * (same & (j >= t))
    for cc in range(2):
        trif[:, 128 + cc] = g * ((np.arange(128) // 64) == cc)
        trib[:, 128 + cc] = g * ((np.arange(128) // 64) == cc)
    c["trif"], c["trib"] = trif, trib
    c["trikf"] = (g * (same & (j > t))).astype(np.float32)
    c["trikb"] = (g * (same & (j < t))).astype(np.float32)
    c["maskf"] = (same & (j <= t)).astype(np.float32)
    c["maskb"] = (same & (j > t)).astype(np.float32)
    c["ident"] = np.eye(128, dtype=np.float32)
    pos = np.arange(4 * SEG)
    row = (pos // 64).astype(np.float32)
    col = (pos % 64).astype(np.float32)
    inv = (10000.0 ** (-np.arange(0, 64, 2, dtype=np.float32) / 64.0)).astype(np.float32)
    ar = row[:, None] * inv
    ac = col[:, None] * inv
    ang = np.concatenate([ar, ar, ac, ac], axis=-1).astype(np.float32)
    sign = np.concatenate([-np.ones(32), np.ones(32), -np.ones(32), np.ones(32)]).astype(np.float32)
    cs = np.stack([np.cos(ang), np.sin(ang) * sign], axis=1).astype(np.float32)
    c["csA"] = np.ascontiguousarray(cs)
    return c


_CACHE = {}


def _get_prog(SEG, upto=99, dbg=False):
    key = (SEG, upto, dbg)
    if key not in _CACHE:
        p = Prog3(SEG, dbg=dbg)
        p.build(upto=upto)
        _CACHE[key] = p
    return _CACHE[key]


def kernel(x_prompt, x_sample, mem_prompt, mem_sample, gla_w_in, gla_w_gate_up, gla_b_gate, gla_norm_g, gla_w_out,
           att_w_qkv, att_q_gain, att_k_gain, att_w_out, mem_w_q, mem_w_kv, mem_w_o, mlp_w1, mlp_w2, ln_g, ln_b,
           _upto=99, _dbg=False):
    f = lambda a: np.ascontiguousarray(np.asarray(a, dtype=np.float32))
    x_prompt, x_sample, mem_prompt, mem_sample = f(x_prompt), f(x_sample), f(mem_prompt), f(mem_sample)
    SEG = x_prompt.shape[1]
    assert x_prompt.shape[0] == 16 and x_sample.shape[0] == 2 and x_sample.shape[1] == 4 * SEG
    prog = _get_prog(SEG, _upto, _dbg)
    cst = _consts(SEG)
    shared = {
        "gla_w_in": f(gla_w_in)[0], "gla_w_gate_up": f(gla_w_gate_up)[0], "gla_b_gate": f(gla_b_gate)[0],
        "gla_norm_g": f(gla_norm_g), "gla_w_out": f(gla_w_out)[0], "att_w_qkv": f(att_w_qkv)[0],
        "att_q_gain": f(att_q_gain), "att_k_gain": f(att_k_gain), "att_w_out": f(att_w_out)[0],
        "mem_w_q": f(mem_w_q), "mem_w_kv": f(mem_w_kv), "mem_w_o": f(mem_w_o), "mlp_w1": f(mlp_w1), "mlp_w2": f(mlp_w2),
        "ln_g": f(ln_g), "ln_b": f(ln_b),
    }
    for k in ("ident", "trif", "trib", "trikf", "trikb", "maskf", "maskb", "csA"):
        shared[k] = cst[k]
    in_maps = []
    for c in range(8):
        sq, qt = c // 4, c % 4
        m = dict(shared)
        m["xA"] = np.ascontiguousarray(np.concatenate([x_prompt[2 * c], x_prompt[2 * c + 1], x_sample[sq]], axis=0))
        m["memA"] = np.ascontiguousarray(np.concatenate([mem_prompt[2 * c], mem_prompt[2 * c + 1], mem_sample[sq]], axis=0))
        m["csOwn"] = np.ascontiguousarray(cst["csA"][qt * SEG:(qt + 1) * SEG])
        om = np.zeros((128, 4), np.float32)
        om[:, qt] = 1.0
        m["ownmask"] = om
        in_maps.append(m)
    res = run_bass_kernel_spmd(prog.nc, in_maps, core_ids=list(range(8)))
    y_prompt = np.zeros((16, SEG, D), np.float32)
    y_sample = np.zeros((2, 4 * SEG, D), np.float32)
    for c in range(8):
        y = res.results[c]["y"]
        y_prompt[2 * c] = y[0:SEG]
        y_prompt[2 * c + 1] = y[SEG:2 * SEG]
        y_sample[c // 4, (c % 4) * SEG:(c % 4 + 1) * SEG] = y[2 * SEG:3 * SEG]
    if _dbg:
        return (y_prompt, y_sample), [r["dbg"] for r in res.results]
    return (y_prompt, y_sample)
```

```python
import contextlib
import numpy as np
import concourse.bass as bass
import concourse.mybir as mybir
from concourse.bass_utils import run_bass_kernel_spmd

F32 = mybir.dt.float32
BF16 = mybir.dt.bfloat16
AF = mybir.ActivationFunctionType
ALU = mybir.AluOpType
AX = mybir.AxisListType

NDMA = 56
SAME_ENGINE_SYNC = True

D = 2048
KC = 16
NMEM = 256
DFF = 8192
EPS = 1e-5
DEPTH = 2
DN_ALPHA = (2.0 * DEPTH) ** 0.25
GIN = 6176


class _Op:
    __slots__ = ("eng", "fn", "deps", "is_dma", "slot", "dval", "pos", "sig", "sigval", "waits", "gw")


class Sched:
    CE = ("pe", "act", "dve", "pool")

    def __init__(self, nc, stack):
        self.nc = nc
        self.esem = {e: stack.enter_context(nc.semaphore("s_" + e)) for e in self.CE}
        self.ecount = {e: 0 for e in self.CE}
        self.dsem = [stack.enter_context(nc.semaphore("d%d" % i)) for i in range(NDMA)]
        self.dcount = [0] * NDMA
        self.carry = []
        self.gk = {}
        self.gslots = {}
        self._reset_phase()
        self.n_inst = 0

    def _reset_phase(self):
        self.ops = []
        self.last_w = {}
        self.readers = {}
        self.slotmap = {}

    def _deps(self, reads, writes):
        deps = set()
        for k in reads:
            w = self.last_w.get(k)
            if w is not None:
                deps.add(w)
        for k in writes:
            w = self.last_w.get(k)
            if w is not None:
                deps.add(w)
            for r in self.readers.get(k, ()):
                deps.add(r)
        return deps

    def _record(self, idx, reads, writes):
        for k in reads:
            self.readers.setdefault(k, []).append(idx)
        for k in writes:
            self.last_w[k] = idx
            self.readers[k] = []

    def _gw(self, reads):
        return [self.gk[k] for k in reads if (k in self.gk and k not in self.last_w)]

    def op(self, eng, fn, reads=(), writes=()):
        o = _Op()
        o.eng = eng
        o.fn = fn
        o.is_dma = False
        o.slot = None
        o.dval = 0
        o.sig = False
        o.gw = self._gw(reads)
        o.deps = self._deps(reads, writes)
        idx = len(self.ops)
        self.ops.append(o)
        self._record(idx, reads, writes)
        return idx

    def dma(self, queue, fn, reads=(), writes=(), semkey=None, gkey=None):
        o = _Op()
        o.eng = queue
        o.fn = fn
        o.is_dma = True
        if gkey is not None:
            if gkey not in self.gslots:
                self.gslots[gkey] = NDMA - 1 - len(self.gslots)
            o.slot = self.gslots[gkey]
        else:
            if semkey not in self.slotmap:
                self.slotmap[semkey] = len(self.slotmap)
                assert len(self.slotmap) + len(self.gslots) <= NDMA, "too many dma sem keys in phase"
            o.slot = self.slotmap[semkey]
        self.dcount[o.slot] += 16
        o.dval = self.dcount[o.slot]
        if gkey is not None:
            self.gk[gkey] = (o.slot, o.dval)
        o.sig = False
        o.gw = self._gw(reads)
        o.deps = self._deps(reads, writes)
        idx = len(self.ops)
        self.ops.append(o)
        self._record(idx, reads, writes)
        return idx

    def end_phase(self):
        ops = self.ops
        engs = ("pe", "act", "dve", "pool", "sp")
        pos_ctr = {e: 0 for e in engs}
        for o in ops:
            o.pos = pos_ctr[o.eng]
            pos_ctr[o.eng] += 1
        waited_pos = {}
        waited_dma = {}
        for o in ops:
            X = o.eng
            o.waits = []
            for (gs, gv) in o.gw:
                if waited_dma.get((X, gs), 0) >= gv:
                    continue
                waited_dma[(X, gs)] = gv
                o.waits.append(("d", gs, gv))
            for d in sorted(o.deps):
                Dd = ops[d]
                if Dd.is_dma:
                    if waited_dma.get((X, Dd.slot), 0) >= Dd.dval:
                        continue
                    waited_dma[(X, Dd.slot)] = Dd.dval
                    o.waits.append(("d", Dd.slot, Dd.dval))
                else:
                    Y = Dd.eng
                    if Y == X and (X == "pe" or not SAME_ENGINE_SYNC):
                        continue
                    if waited_pos.get((X, Y), -1) >= Dd.pos:
                        continue
                    waited_pos[(X, Y)] = Dd.pos
                    Dd.sig = True
                    o.waits.append(("e", Y, d))
        last = {}
        for i, o in enumerate(ops):
            if not o.is_dma:
                last[o.eng] = i
        for e, i in last.items():
            ops[i].sig = True
        for o in ops:
            if (not o.is_dma) and o.sig:
                self.ecount[o.eng] += 1
                o.sigval = self.ecount[o.eng]
        carry = self.carry
        sched = self

        def stream(X):
            def body(eng):
                for (sem, val) in carry:
                    eng.wait_ge(sem, val)
                for o in ops:
                    if o.eng != X:
                        continue
                    for w in o.waits:
                        if w[0] == "d":
                            eng.wait_ge(sched.dsem[w[1]], w[2])
                        else:
                            eng.wait_ge(sched.esem[w[1]], ops[w[2]].sigval)
                    inst = o.fn(eng)
                    sched.n_inst += 1
                    if o.is_dma:
                        inst.then_inc(sched.dsem[o.slot], 16)
                    elif o.sig:
                        inst.then_inc(sched.esem[X], 1)
            return body

        with self.nc.Block() as block:
            block.tensor(stream("pe"))
            block.scalar(stream("act"))
            block.vector(stream("dve"))
            block.gpsimd(stream("pool"))
            block.sync(stream("sp"))
        newc = []
        for e in self.CE:
            if self.ecount[e] > 0:
                newc.append((self.esem[e], self.ecount[e]))
        gset = set(self.gslots.values())
        for s in range(NDMA):
            if self.dcount[s] > 0 and s not in gset:
                newc.append((self.dsem[s], self.dcount[s]))
        self.carry = newc
        self._reset_phase()


class Prog:
    def __init__(self, SEG, dbg=False):
        self.SEG = SEG
        self.TB = min(512, SEG)
        self.NT = self.TB // 128
        self.NBS = SEG // self.TB
        self.NSA = 6
        self.NSB = 3
        self.dbg = dbg
        self.nc = bass.Bass("TRN2", target_bir_lowering=False)
        self._uid = 0

    def din(self, name, shape, dt=F32):
        return self.nc.dram_tensor(name, list(shape), dt, kind="ExternalInput").ap()

    def dscr(self, name, shape, dt):
        return self.nc.dram_tensor(name, list(shape), dt, kind="Internal").ap()

    def dout(self, name, shape, dt=F32):
        return self.nc.dram_tensor(name, list(shape), dt, kind="ExternalOutput").ap()

    def sb(self, st, name, shape, dt):
        self._uid += 1
        return st.enter_context(self.nc.sbuf_tensor("%s_%d" % (name, self._uid), list(shape), dt))

    def ps(self, st, name, shape, dt=F32):
        self._uid += 1
        return st.enter_context(self.nc.psum_tensor("%s_%d" % (name, self._uid), list(shape), dt))

    def mmg(self, out, pairs, reads, writes, first=True, last=True):
        pairs = list(pairs)

        def fn(e):
            n = len(pairs)
            i = None
            for j, (l, r) in enumerate(pairs):
                i = e.matmul(out, lhsT=l, rhs=r, start=(first and j == 0), stop=(last and j == n - 1))
            return i
        return self.S.op("pe", fn, reads=reads, writes=writes)

    def declare(self):
        SEG, TB = self.SEG, self.TB
        NA = self.NSA * SEG
        NB_ = self.NSB * SEG
        i = {}
        i["xA"] = self.din("xA", [NA, D])
        i["memA"] = self.din("memA", [3 * NMEM, D])
        i["gla_w_in"] = self.din("gla_w_in", [D, GIN])
        i["gla_w_gate_up"] = self.din("gla_w_gate_up", [2, 16, 1024])
        i["gla_b_gate"] = self.din("gla_b_gate", [2, 1024])
        i["gla_norm_g"] = self.din("gla_norm_g", [1, 512])
        i["gla_w_out"] = self.din("gla_w_out", [D, D])
        i["att_w_qkv"] = self.din("att_w_qkv", [D, 3072])
        i["att_q_gain"] = self.din("att_q_gain", [1, 128])
        i["att_k_gain"] = self.din("att_k_gain", [1, 128])
        i["att_w_out"] = self.din("att_w_out", [D, D])
        i["mem_w_q"] = self.din("mem_w_q", [2, D, D])
        i["mem_w_kv"] = self.din("mem_w_kv", [2, D, 2 * D])
        i["mem_w_o"] = self.din("mem_w_o", [2, D, D])
        i["mlp_w1"] = self.din("mlp_w1", [2, D, DFF])
        i["mlp_w2"] = self.din("mlp_w2", [2, DFF, D])
        i["ln_g"] = self.din("ln_g", [2, 3, D])
        i["ln_b"] = self.din("ln_b", [2, 3, D])
        i["ident"] = self.din("ident", [128, 128])
        i["trif"] = self.din("trif", [128, 130])
        i["trib"] = self.din("trib", [128, 130])
        i["trikf"] = self.din("trikf", [128, 128])
        i["trikb"] = self.din("trikb", [128, 128])
        i["maskf"] = self.din("maskf", [128, 128])
        i["maskb"] = self.din("maskb", [128, 128])
        i["csA"] = self.din("csA", [4 * SEG, 2, 128])
        i["csOwn"] = self.din("csOwn", [SEG, 2, 128])
        i["ownmask"] = self.din("ownmask", [128, 4])
        self.i = i
        s = {}
        s["wb_gin"] = self.dscr("wb_gin", [D, GIN], BF16)
        s["wb_gout"] = self.dscr("wb_gout", [D, D], BF16)
        s["wb_qkv"] = self.dscr("wb_qkv", [D, 3072], BF16)
        s["wb_aout"] = self.dscr("wb_aout", [D, D], BF16)
        s["wb_mq"] = self.dscr("wb_mq", [2, D, D], BF16)
        s["wb_mkv"] = self.dscr("wb_mkv", [2, D, 2 * D], BF16)
        s["wb_mo"] = self.dscr("wb_mo", [2, D, D], BF16)
        s["wb_w1"] = self.dscr("wb_w1", [2, D, DFF], BF16)
        s["wb_w2"] = self.dscr("wb_w2", [2, DFF, D], BF16)
        NBA = NA // TB
        NBB = NB_ // TB
        for nm in ("XTa", "XTb", "OT"):
            s[nm] = self.dscr(nm, [NBA, 128, KC, TB], BF16)
        for nm in ("XFa", "XFb"):
            s[nm] = self.dscr(nm, [NA, D], F32)
        s["XT1"] = self.dscr("XT1", [NBB, 128, KC, TB], BF16)
        s["XF1"] = self.dscr("XF1", [NB_, D], F32)
        s["SBst"] = self.dscr("SBst", [4 * SEG // 64, 128, 2, 512], BF16)
        s["KVst"] = self.dscr("KVst", [4 * SEG // 128, 128, 768], BF16)
        s["MK"] = self.dscr("MK", [2, 128, KC, 3 * NMEM], BF16)
        s["MV"] = self.dscr("MV", [2, 128, 6, D], BF16)
        s["QT"] = self.dscr("QT", [NBB, 128, KC, TB], BF16)
        s["KT"] = self.dscr("KT", [128, 4, NA], BF16)
        s["VV"] = self.dscr("VV", [NA, 512], BF16)
        self.s = s
        self.y = self.dout("y", [NB_, D])
        if self.dbg:
            self.dbgout = self.dout("dbg", [NA, D])

    def phase_weights(self, part):
        S = self.S
        i, s = self.i, self.s

        def conv(dst, src, rows, key):
            cols = src.shape[-1]
            nch = max(1, rows // 256)
            rs = rows // nch
            for c in range(nch):
                if cols > 2048 and cols % 2048 == 0:
                    o_ = dst[c * rs:(c + 1) * rs, :].rearrange("r (a b) -> r a b", b=2048)
                    i_ = src[c * rs:(c + 1) * rs, :].rearrange("r (a b) -> r a b", b=2048)
                else:
                    o_ = dst[c * rs:(c + 1) * rs, :]
                    i_ = src[c * rs:(c + 1) * rs, :]
                kw = {} if cols <= 2048 or cols % 2048 == 0 else {"max_dma_last_dim": 4096}
                S.dma("pool", lambda e, o_=o_, i_=i_, kw=kw: e.dma_start(out=o_, in_=i_, **kw),
                      gkey=key)
        if part == 0:
            conv(s["wb_gin"], i["gla_w_in"], D, "wb_gin")
            for l in range(2):
                conv(s["wb_mkv"][l], i["mem_w_kv"][l], D, "wb_mkv")
            return
        conv(s["wb_gout"], i["gla_w_out"], D, "wb_gout")
        conv(s["wb_mq"][0], i["mem_w_q"][0], D, "wb_mq")
        conv(s["wb_mo"][0], i["mem_w_o"][0], D, "wb_mo")
        conv(s["wb_w1"][0], i["mlp_w1"][0], D, "wb_w1")
        conv(s["wb_w2"][0], i["mlp_w2"][0], DFF, "wb_w2")
        conv(s["wb_qkv"], i["att_w_qkv"], D, "wb_qkv")
        conv(s["wb_aout"], i["att_w_out"], D, "wb_aout")
        conv(s["wb_mq"][1], i["mem_w_q"][1], D, "wb_mq")
        conv(s["wb_mo"][1], i["mem_w_o"][1], D, "wb_mo")
        conv(s["wb_w1"][1], i["mlp_w1"][1], D, "wb_w1")
        conv(s["wb_w2"][1], i["mlp_w2"][1], DFF, "wb_w2")

    def transpose_tile(self, src_fn, src_keys, dst_blk, col, dst_key, idb, pT, ptkeys, ctr, nchunks=KC, dst_c0=0):
        S = self.S
        for g in range(0, nchunks, 8):
            n = min(8, nchunks - g)
            b = ctr[0] % len(ptkeys)
            ctr[0] += 1

            def tp(e, g=g, n=n, b=b):
                ins = None
                for k in range(n):
                    ins = e.transpose(out=pT[b][:, k, :], in_=src_fn(g + k), identity=idb[:])
                return ins
            S.op("pe", tp, reads=list(src_keys) + ["idb"], writes=[ptkeys[b]])
            eng = "act" if (ctr[0] % 2) else "dve"
            if eng == "act":
                S.op("act", lambda e, g=g, n=n, b=b: e.copy(out=dst_blk[:, dst_c0 + g:dst_c0 + g + n, col:col + 128], in_=pT[b][:, 0:n, :]),
                     reads=[ptkeys[b]], writes=[dst_key])
            else:
                S.op("dve", lambda e, g=g, n=n, b=b: e.tensor_copy(out=dst_blk[:, dst_c0 + g:dst_c0 + g + n, col:col + 128], in_=pT[b][:, 0:n, :]),
                     reads=[ptkeys[b]], writes=[dst_key])

    def load_consts(self, st, want):
        S = self.S
        c = {}
        if "ident" in want:
            idf = self.sb(st, "idf", [128, 128], F32)
            idb = self.sb(st, "idb", [128, 128], BF16)
            S.dma("sp", lambda e: e.dma_start(out=idf[:], in_=self.i["ident"]), writes=["idf"], semkey="idf")
            S.op("dve", lambda e: e.tensor_copy(out=idb[:], in_=idf[:]), reads=["idf"], writes=["idb"])
            c["idb"] = idb
        if "ones" in want:
            ones = self.sb(st, "ones", [128, 128], BF16)
            S.op("pool", lambda e: e.memset(ones[:], 1.0), writes=["ones"])
            c["ones"] = ones
        return c

    def phase_prep(self, xsrc, XT, ntok):
        S = self.S
        TB, NT = self.TB, self.NT
        with contextlib.ExitStack() as st:
            c = self.load_consts(st, ["ident"])
            idb = c["idb"]
            xin = [self.sb(st, "xin", [128, D], F32) for _ in range(2)]
            xbf = [self.sb(st, "xbf", [128, D], BF16) for _ in range(2)]
            blk = [self.sb(st, "blk", [128, KC, TB], BF16) for _ in range(2)]
            pT = [self.ps(st, "pT", [128, 8, 128], BF16) for _ in range(4)]
            ptk = [("pT", j) for j in range(4)]
            ctr = [0]
            for b in range(ntok // TB):
                bb = b % 2
                for t in range(NT):
                    g = b * NT + t
                    p = g % 2
                    S.dma("sp", lambda e, g=g, p=p: e.dma_start(out=xin[p][:], in_=xsrc[g * 128:(g + 1) * 128, :]),
                          writes=[("xin", p)], semkey=("xin", p))
                    S.op("act" if g % 2 else "dve",
                         (lambda e, p=p: e.copy(out=xbf[p][:], in_=xin[p][:])) if g % 2 else
                         (lambda e, p=p: e.tensor_copy(out=xbf[p][:], in_=xin[p][:])),
                         reads=[("xin", p)], writes=[("xbf", p)])
                    self.transpose_tile(lambda k, p=p: xbf[p][:, k * 128:(k + 1) * 128], [("xbf", p)], blk[bb], t * 128,
                                        ("blk", bb), idb, pT, ptk, ctr)
                S.dma("sp", lambda e, b=b, bb=bb: e.dma_start(out=XT[b], in_=blk[bb][:]),
                      reads=[("blk", bb)], writes=[("XT", b)], semkey=("blkst", bb))
            self.S.end_phase()

    def ln_tile(self, li, h, hkey, L):
        S = self.S
        p = li % 2
        pb = li % len(L["xob"])
        stats, mv, rstd = L["stats"][p], L["mv"][p], L["rstd"][p]
        px = li % len(L["xo"])
        xo, xob = L["xo"][px], L["xob"][pb]

        def bn(e):
            ins = None
            for j in range(4):
                ins = e.bn_stats(out=stats[:, j, :], in_=h[:, j * 512:(j + 1) * 512])
            return ins
        S.op("dve", bn, reads=[hkey], writes=[("stats", p)])
        S.op("dve", lambda e: e.bn_aggr(out=mv[:], in_=stats[:].rearrange("p a b -> p (a b)")),
             reads=[("stats", p)], writes=[("mv", p)])
        S.op("act", lambda e: e.activation(out=rstd[:], in_=mv[:, 1:2], func=AF.Ln, bias=L["epsc"][:], scale=1.0),
             reads=[("mv", p), "epsc"], writes=[("rstd", p)])
        S.op("act", lambda e: e.activation(out=rstd[:], in_=rstd[:], func=AF.Exp, scale=-0.5),
             reads=[("rstd", p)], writes=[("rstd", p)])
        S.op("dve", lambda e: e.scalar_tensor_tensor(out=xo[:], in0=h[:], scalar=mv[:, 0:1], in1=L["gbc"][:],
                                                     op0=ALU.subtract, op1=ALU.mult),
             reads=[hkey, ("mv", p), "gbc"], writes=[("xo", px)])
        S.op("dve", lambda e: e.scalar_tensor_tensor(out=xo[:], in0=xo[:], scalar=rstd[:], in1=L["bbc"][:],
                                                     op0=ALU.mult, op1=ALU.add),
             reads=[("xo", px), ("rstd", p), "bbc"], writes=[("xo", px)])
        S.op("act", lambda e: e.copy(out=xob[:], in_=xo[:]), reads=[("xo", px)], writes=[("xob", pb)])
        return px, pb

    def ln_alloc(self, st, layer, which, nxob=2, nxo=2):
        S = self.S
        L = {}
        L["stats"] = [self.sb(st, "stats", [128, 4, 6], F32) for _ in range(2)]
        L["mv"] = [self.sb(st, "mv", [128, 2], F32) for _ in range(2)]
        L["rstd"] = [self.sb(st, "rstd", [128, 1], F32) for _ in range(2)]
        L["xo"] = [self.sb(st, "xo", [128, D], F32) for _ in range(nxo)]
        L["xob"] = [self.sb(st, "xob", [128, D], BF16) for _ in range(nxob)]
        L["gbc"] = self.sb(st, "gbc", [128, D], F32)
        L["bbc"] = self.sb(st, "bbc", [128, D], F32)
        L["epsc"] = self.sb(st, "epsc", [128, 1], F32)
        S.op("pool", lambda e: e.memset(L["epsc"][:], EPS), writes=["epsc"])
        g = self.i["ln_g"][layer, which:which + 1, :]
        b = self.i["ln_b"][layer, which:which + 1, :]
        S.dma("sp", lambda e: e.dma_start(out=L["gbc"][:], in_=g.partition_broadcast(128)), writes=["gbc"], semkey="gbc")
        S.dma("sp", lambda e: e.dma_start(out=L["bbc"][:], in_=b.partition_broadcast(128)), writes=["bbc"], semkey="bbc")
        return L

    def phase_proj_ln(self, Wb, wkey, OT, XFin, XFout, XTout, ntok, layer, which, final_out=None):
        S = self.S
        TB, NT = self.TB, self.NT
        with contextlib.ExitStack() as st:
            c = self.load_consts(st, ["ident"])
            idb = c["idb"]
            L = self.ln_alloc(st, layer, which)
            W = self.sb(st, "W", [128, KC, D], BF16)
            for q in range(4):
                S.dma("sp", lambda e, q=q: e.dma_start(out=W[:, q * 4:(q + 1) * 4, :],
                                                       in_=Wb[q * 512:(q + 1) * 512, :].rearrange("(k p) n -> p k n", p=128)),
                      reads=[wkey], writes=[("W", q)], semkey=("W", q))
            ot = [self.sb(st, "ot", [128, KC, TB], BF16) for _ in range(2)]
            xf = [self.sb(st, "xf", [128, D], F32) for _ in range(2)]
            hb = [self.sb(st, "hb", [128, D], F32) for _ in range(2)]
            blk = [self.sb(st, "blk", [128, KC, TB], BF16) for _ in range(2)] if XTout is not None else None
            pY = self.ps(st, "pY", [128, 4, 512], F32)
            pT = [self.ps(st, "pT", [128, 8, 128], BF16) for _ in range(4)]
            ptk = [("pT", j) for j in range(4)]
            ctr = [0]
            deferred = []
            for b in range(ntok // TB):
                bb = b % 2
                S.dma("sp", lambda e, b=b, bb=bb: e.dma_start(out=ot[bb][:], in_=OT[b]),
                      reads=[("OT", b)], writes=[("ot", bb)], semkey=("ot", bb))
                for t in range(NT):
                    g = b * NT + t
                    p = g % 2
                    S.dma("sp", lambda e, g=g, p=p: e.dma_start(out=xf[p][:], in_=XFin[g * 128:(g + 1) * 128, :]),
                          reads=[("XF", g)], writes=[("xf", p)], semkey=("xf", p))
                    for n in range(4):
                        self.mmg(pY[:, n, :], [(ot[bb][:, k, t * 128:(t + 1) * 128], W[:, k, n * 512:(n + 1) * 512]) for k in range(KC)],
                                 reads=[("ot", bb)] + [("W", q) for q in range(4)], writes=[("pY", n)])
                        S.op("dve", lambda e, n=n, p=p: e.scalar_tensor_tensor(
                            out=hb[p][:, n * 512:(n + 1) * 512], in0=xf[p][:, n * 512:(n + 1) * 512], scalar=DN_ALPHA,
                            in1=pY[:, n, :], op0=ALU.mult, op1=ALU.add),
                            reads=[("xf", p), ("pY", n)], writes=[("hb", p)])
                    lp, lpb = self.ln_tile(g, hb[p], ("hb", p), L)
                    dst = final_out if final_out is not None else XFout
                    S.dma("pool", lambda e, g=g, lp=lp, dst=dst: e.dma_start(out=dst[g * 128:(g + 1) * 128, :], in_=L["xo"][lp][:]),
                          reads=[("xo", lp)], writes=[("XFo", g)], semkey=("xost", lp))
                    for fn_ in deferred:
                        fn_()
                    deferred = []
                    if XTout is not None:
                        def dtr(lp=lpb, bb=bb, t=t, b=b):
                            self.transpose_tile(lambda k, lp=lp: L["xob"][lp][:, k * 128:(k + 1) * 128], [("xob", lp)], blk[bb],
                                                t * 128, ("blk", bb), idb, pT, ptk, ctr)
                            if t == NT - 1:
                                S.dma("pool", lambda e, b=b, bb=bb: e.dma_start(out=XTout[b], in_=blk[bb][:]),
                                      reads=[("blk", bb)], writes=[("XTo", b)], semkey=("blkst", bb))
                        deferred.append(dtr)
            for fn_ in deferred:
                fn_()
            S.end_phase()

    def phase_memkv(self):
        S = self.S
        i, s = self.i, self.s
        NM = 3 * NMEM
        with contextlib.ExitStack() as st:
            c = self.load_consts(st, ["ident"])
            idb = c["idb"]
            xin = [self.sb(st, "xin", [128, D], F32) for _ in range(2)]
            xbf = [self.sb(st, "xbf", [128, D], BF16) for _ in range(2)]
            memT = self.sb(st, "memT", [128, KC, NM], BF16)
            wg = [self.sb(st, "wg", [128, KC, 512], BF16) for _ in range(2)]
            mk = self.sb(st, "mk", [128, KC, NM], BF16)
            mvv = self.sb(st, "mvv", [128, 6, D], BF16)
            pT = [self.ps(st, "pT", [128, 8, 128], BF16) for _ in range(2)]
            ptk = [("pT", j) for j in range(2)]
            pA = [self.ps(st, "pA", [128, 512], F32) for _ in range(2)]
            ctr = [0]
            for g in range(6):
                p = g % 2
                S.dma("sp", lambda e, g=g, p=p: e.dma_start(out=xin[p][:], in_=i["memA"][g * 128:(g + 1) * 128, :]),
                      writes=[("xin", p)], semkey=("xin", p))
                S.op("dve", lambda e, p=p: e.tensor_copy(out=xbf[p][:], in_=xin[p][:]), reads=[("xin", p)], writes=[("xbf", p)])
                self.transpose_tile(lambda k, p=p: xbf[p][:, k * 128:(k + 1) * 128], [("xbf", p)], memT, g * 128,
                                    "memT", idb, pT, ptk, ctr)
            ac = 0
            wc = 0
            for l in range(2):
                for gi in range(8):
                    wp = wc % 2
                    wc += 1
                    S.dma("sp", lambda e, l=l, gi=gi, wp=wp: e.dma_start(
                        out=wg[wp][:], in_=s["wb_mkv"][l][:, gi * 512:(gi + 1) * 512].rearrange("(k p) n -> p k n", p=128)),
                        reads=["wb_mkv"], writes=[("wg", wp)], semkey=("wg", wp))
                    if gi < 4:
                        for fc4 in range(4):
                            fc = gi * 4 + fc4
                            for (c0, cn) in ((0, 512), (512, NM - 512)):
                                a = ac % 2
                                ac += 1
                                self.mmg(pA[a][:, 0:cn], [(wg[wp][:, k, fc4 * 128:(fc4 + 1) * 128], memT[:, k, c0:c0 + cn]) for k in range(KC)],
                                         reads=[("wg", wp), "memT"], writes=[("pA", a)])
                                S.op("act", lambda e, a=a, fc=fc, c0=c0, cn=cn: e.copy(out=mk[:, fc, c0:c0 + cn], in_=pA[a][:, 0:cn]),
                                     reads=[("pA", a)], writes=["mk"])
                    else:
                        n = gi - 4
                        for mt in range(6):
                            a = ac % 2
                            ac += 1
                            self.mmg(pA[a][:, :], [(memT[:, k, mt * 128:(mt + 1) * 128], wg[wp][:, k, :]) for k in range(KC)],
                                     reads=[("wg", wp), "memT"], writes=[("pA", a)])
                            S.op("dve", lambda e, a=a, mt=mt, n=n: e.tensor_copy(out=mvv[:, mt, n * 512:(n + 1) * 512], in_=pA[a][:, :]),
                                 reads=[("pA", a)], writes=["mvv"])
                S.dma("sp", lambda e, l=l: e.dma_start(out=s["MK"][l], in_=mk[:]), reads=["mk"], writes=[("MK", l)], semkey="mkst")
                S.dma("sp", lambda e, l=l: e.dma_start(out=s["MV"][l], in_=mvv[:]), reads=["mvv"], writes=[("MV", l)], semkey="mvst")
            S.end_phase()

    def phase_xattn(self, layer, XT, OT, nseg, memidx):
        S = self.S
        s = self.s
        TB, NBS = self.TB, self.NBS
        sc = 512 ** -0.5
        with contextlib.ExitStack() as st:
            c = self.load_consts(st, ["ones"])
            ones = c["ones"]
            W = self.sb(st, "W", [128, KC, D], BF16)
            for q in range(4):
                S.dma("sp", lambda e, q=q: e.dma_start(out=W[:, q * 4:(q + 1) * 4, :],
                                                       in_=s["wb_mq"][layer][q * 512:(q + 1) * 512, :].rearrange("(k p) n -> p k n", p=128)),
                      reads=["wb_mq"], writes=[("W", q)], semkey=("W", q))
            mk = self.sb(st, "mk", [128, KC, NMEM], BF16)
            mvv = self.sb(st, "mvv", [128, 2, D], BF16)
            xt = [self.sb(st, "xt", [128, KC, TB], BF16) for _ in range(2)]
            qT = self.sb(st, "qT", [128, KC, TB], BF16)
            ob = [self.sb(st, "ob", [128, KC, TB], BF16) for _ in range(2)]
            PT = [self.sb(st, "PT", [128, 2, TB], BF16) for _ in range(2)]
            rden = [self.sb(st, "rden", [128, TB], F32) for _ in range(2)]
            pQ = [self.ps(st, "pQ", [128, 512], F32) for _ in range(2)]
            pS = [self.ps(st, "pS", [128, 512], F32) for _ in range(2)]
            pD = self.ps(st, "pD", [128, 512], F32)
            pO = [self.ps(st, "pO", [128, 512], F32) for _ in range(2)]
            qc = 0
            sc_ = 0
            oc = 0
            for sg in range(nseg):
                m = memidx[sg]
                S.dma("sp", lambda e, m=m: e.dma_start(out=mk[:], in_=s["MK"][layer][:, :, m * NMEM:(m + 1) * NMEM]),
                      reads=[("MK", layer)], writes=["mk"], semkey="mk")
                S.dma("sp", lambda e, m=m: e.dma_start(out=mvv[:], in_=s["MV"][layer][:, 2 * m:2 * m + 2, :]),
                      reads=[("MV", layer)], writes=["mvv"], semkey="mvv")
                for bs in range(NBS):
                    b = sg * NBS + bs
                    bb = b % 2
                    S.dma("sp", lambda e, b=b, bb=bb: e.dma_start(out=xt[bb][:], in_=XT[b]),
                          reads=[("XT", b)], writes=[("xt", bb)], semkey=("xt", bb))
                    for fc in range(KC):
                        a = qc % 2
                        qc += 1
                        self.mmg(pQ[a][:, 0:TB], [(W[:, k, fc * 128:(fc + 1) * 128], xt[bb][:, k, :]) for k in range(KC)],
                                 reads=[("xt", bb)] + [("W", q) for q in range(4)], writes=[("pQ", a)])
                        if fc % 2:
                            S.op("act", lambda e, a=a, fc=fc: e.copy(out=qT[:, fc, :], in_=pQ[a][:, 0:TB]),
                                 reads=[("pQ", a)], writes=[("qT", fc // 4)])
                        else:
                            S.op("dve", lambda e, a=a, fc=fc: e.tensor_copy(out=qT[:, fc, :], in_=pQ[a][:, 0:TB]),
                                 reads=[("pQ", a)], writes=[("qT", fc // 4)])
                    for h in range(4):
                        pp = (b * 4 + h) % 2
                        for kt in range(2):
                            a = sc_ % 2
                            sc_ += 1
                            self.mmg(pS[a][:, 0:TB], [(mk[:, h * 4 + c4, kt * 128:(kt + 1) * 128], qT[:, h * 4 + c4, :]) for c4 in range(4)],
                                     reads=["mk", ("qT", h)], writes=[("pS", a)])
                            S.op("act", lambda e, a=a, kt=kt, pp=pp: e.activation(out=PT[pp][:, kt, :], in_=pS[a][:, 0:TB], func=AF.Exp, scale=sc),
                                 reads=[("pS", a)], writes=[("PT", pp)])
                        self.mmg(pD[:, 0:TB], [(ones[:], PT[pp][:, kt, :]) for kt in range(2)],
                                 reads=["ones", ("PT", pp)], writes=["pD"])
                        S.op("dve", lambda e, pp=pp: e.reciprocal(out=rden[pp][:], in_=pD[:, 0:TB]),
                             reads=["pD"], writes=[("rden", pp)])
                        for c4 in range(4):
                            a = oc % 2
                            oc += 1
                            self.mmg(pO[a][:, 0:TB], [(mvv[:, kt, h * 512 + c4 * 128:h * 512 + (c4 + 1) * 128], PT[pp][:, kt, :]) for kt in range(2)],
                                     reads=["mvv", ("PT", pp)], writes=[("pO", a)])
                            S.op("dve", lambda e, a=a, h=h, c4=c4, bb=bb, pp=pp: e.tensor_tensor(
                                out=ob[bb][:, h * 4 + c4, :], in0=pO[a][:, 0:TB], in1=rden[pp][:], op=ALU.mult),
                                reads=[("pO", a), ("rden", pp)], writes=[("ob", bb)])
                    S.dma("pool", lambda e, b=b, bb=bb: e.dma_start(out=OT[b], in_=ob[bb][:]),
                          reads=[("ob", bb)], writes=[("OT", b)], semkey=("obst", bb))
            S.end_phase()

    def phase_mlp(self, layer, XT, XFin, XFout, XTout, ntok, final_out=None):
        S = self.S
        s = self.s
        TB, NT = self.TB, self.NT
        FG = 512
        NG = DFF // FG
        w1b = s["wb_w1"][layer]
        w2b = s["wb_w2"][layer]
        with contextlib.ExitStack() as st:
            c = self.load_consts(st, ["ident"])
            idb = c["idb"]
            L = self.ln_alloc(st, layer, 2, nxob=NT, nxo=1)
            xt = [self.sb(st, "xt", [128, KC, TB], BF16) for _ in range(2)]
            w1 = [self.sb(st, "w1", [128, KC, FG], BF16) for _ in range(2)]
            w2 = [self.sb(st, "w2", [128, 4, D], BF16) for _ in range(2)]
            hT = [self.sb(st, "hT", [128, 4, TB], BF16) for _ in range(2)]
            rl = [self.sb(st, "rl", [128, TB], F32) for _ in range(2)]
            ys = self.sb(st, "ys", [128, NT, D], F32)
            xf = [self.sb(st, "xf", [128, D], F32) for _ in range(1)]
            blk = [self.sb(st, "blk", [128, KC, TB], BF16) for _ in range(1)] if XTout is not None else None
            pH = [self.ps(st, "pH", [128, 512], F32) for _ in range(2)]
            pY = [self.ps(st, "pY", [128, 512], F32) for _ in range(4)]
            pT = [self.ps(st, "pT", [128, 8, 128], BF16) for _ in range(2)]
            ptk = [("pT", j) for j in range(2)]
            ctr = [0]
            hc = 0
            yc = 0
            wc = 0
            deferred = []
            nblk = ntok // TB
            pre = {}

            def load_x(b):
                if ("x", b) in pre or b >= nblk:
                    return
                pre[("x", b)] = 1
                bb_ = b % 2
                S.dma("sp", lambda e, b=b, bb_=bb_: e.dma_start(out=xt[bb_][:], in_=XT[b]),
                      reads=[("XT", b)], writes=[("xt", bb_)], semkey=("xt", bb_))

            def load_w(b, g):
                if b >= nblk:
                    return None
                if ("w", b, g) in pre:
                    return pre[("w", b, g)]
                wp = (b * NG + g) % 2
                pre[("w", b, g)] = wp
                S.dma("sp", lambda e, g=g, wp=wp: e.dma_start(
                    out=w1[wp][:], in_=w1b[:, g * FG:(g + 1) * FG].rearrange("(k p) n -> p k n", p=128)),
                    reads=["wb_w1"], writes=[("w1", wp)], semkey=("w1", wp))
                S.dma("sp", lambda e, g=g, wp=wp: e.dma_start(
                    out=w2[wp][:], in_=w2b[g * FG:(g + 1) * FG, :].rearrange("(k p) n -> p k n", p=128)),
                    reads=["wb_w2"], writes=[("w2", wp)], semkey=("w2", wp))
                return wp

            for b in range(nblk):
                bb = b % 2
                load_x(b)
                for g in range(NG):
                    wp = load_w(b, g)
                    if g == NG - 1:
                        load_x(b + 1)
                        load_w(b + 1, 0)
                    hp = g % 2
                    if g == 1:
                        for fn_ in deferred:
                            fn_()
                        deferred = []
                    for fc in range(4):
                        a = hc % 2
                        hc += 1
                        self.mmg(pH[a][:, 0:TB], [(w1[wp][:, k, fc * 128:(fc + 1) * 128], xt[bb][:, k, :]) for k in range(KC)],
                                 reads=[("w1", wp), ("xt", bb)], writes=[("pH", a)])
                        S.op("act", lambda e, a=a: e.activation(out=rl[a][:], in_=pH[a][:, 0:TB], func=AF.Relu),
                             reads=[("pH", a)], writes=[("rl", a)])
                        S.op("pool", lambda e, a=a, hp=hp, fc=fc: e.tensor_tensor(out=hT[hp][:, fc, :], in0=rl[a][:], in1=rl[a][:], op=ALU.mult),
                             reads=[("rl", a)], writes=[("hT", hp)])
                    for t in range(NT):
                        for n in range(4):
                            a = yc % 4
                            yc += 1
                            self.mmg(pY[a][:, :], [(hT[hp][:, fc, t * 128:(t + 1) * 128], w2[wp][:, fc, n * 512:(n + 1) * 512]) for fc in range(4)],
                                     reads=[("hT", hp), ("w2", wp)], writes=[("pY", a)])
                            if g == 0:
                                S.op("dve", lambda e, a=a, t=t, n=n: e.tensor_copy(out=ys[:, t, n * 512:(n + 1) * 512], in_=pY[a][:, :]),
                                     reads=[("pY", a)], writes=[("ys", t)])
                            else:
                                S.op("dve", lambda e, a=a, t=t, n=n: e.tensor_tensor(out=ys[:, t, n * 512:(n + 1) * 512],
                                                                                    in0=ys[:, t, n * 512:(n + 1) * 512], in1=pY[a][:, :], op=ALU.add),
                                     reads=[("pY", a), ("ys", t)], writes=[("ys", t)])
                for t in range(NT):
                    g_ = b * NT + t
                    p = 0
                    S.dma("sp", lambda e, g_=g_, p=p: e.dma_start(out=xf[p][:], in_=XFin[g_ * 128:(g_ + 1) * 128, :]),
                          reads=[("XF", g_)], writes=[("xf", p)], semkey=("xf", p))
                    S.op("dve", lambda e, t=t, p=p: e.scalar_tensor_tensor(out=ys[:, t, :], in0=xf[p][:], scalar=DN_ALPHA, in1=ys[:, t, :],
                                                                          op0=ALU.mult, op1=ALU.add),
                         reads=[("xf", p), ("ys", t)], writes=[("ys", t)])
                    lp, lpb = self.ln_tile(g_, ys[:, t, :], ("ys", t), L)
                    dst = final_out if final_out is not None else XFout
                    S.dma("sp", lambda e, g_=g_, lp=lp, dst=dst: e.dma_start(out=dst[g_ * 128:(g_ + 1) * 128, :], in_=L["xo"][lp][:]),
                          reads=[("xo", lp)], writes=[("XFo", g_)], semkey=("xost", lp))
                    if XTout is not None:
                        def dtr(lp=lpb, t=t, b=b):
                            self.transpose_tile(lambda k, lp=lp: L["xob"][lp][:, k * 128:(k + 1) * 128], [("xob", lp)], blk[0],
                                                t * 128, ("blk", 0), idb, pT, ptk, ctr)
                            if t == NT - 1:
                                S.dma("sp", lambda e, b=b: e.dma_start(out=XTout[b], in_=blk[0][:]),
                                      reads=[("blk", 0)], writes=[("XTo", b)], semkey=("blkst", 0))
                        deferred.append(dtr)
            for fn_ in deferred:
                fn_()
            S.end_phase()

    def phase_select(self, XFa, XTa, XF1, XT1):
        S = self.S
        SEG, TB, NBS = self.SEG, self.TB, self.NBS
        with contextlib.ExitStack() as st:
            om = self.sb(st, "om", [128, 4], F32)
            S.dma("sp", lambda e: e.dma_start(out=om[:], in_=self.i["ownmask"]), writes=["om"], semkey="om")
            cf = [self.sb(st, "cf", [128, D], F32) for _ in range(4)]
            af = [self.sb(st, "af", [128, D], F32) for _ in range(2)]
            cb = [self.sb(st, "cb", [128, KC * TB], BF16) for _ in range(4)]
            ab = [self.sb(st, "ab", [128, KC * TB], BF16) for _ in range(2)]
            TTs = SEG // 128
            S.dma("sp", lambda e: e.dma_start(out=XF1[0:2 * SEG, :], in_=XFa[0:2 * SEG, :]), reads=["XFall"], writes=["XF1"], semkey="cp1")
            S.dma("sp", lambda e: e.dma_start(out=XT1[0:2 * NBS], in_=XTa[0:2 * NBS]), reads=["XTall"], writes=["XT1"], semkey="cp2")
            for t in range(TTs):
                p = t % 2
                for j in range(4):
                    r0 = (2 + j) * SEG + t * 128
                    S.dma("sp", lambda e, j=j, r0=r0: e.dma_start(out=cf[j][:], in_=XFa[r0:r0 + 128, :]),
                          reads=["XFall"], writes=[("cf", j)], semkey=("cf", j))
                S.op("dve", lambda e, p=p: e.tensor_scalar(out=af[p][:], in0=cf[0][:], scalar1=om[:, 0:1], scalar2=None, op0=ALU.mult),
                     reads=[("cf", 0), "om"], writes=[("af", p)])
                for j in range(1, 4):
                    S.op("dve", lambda e, p=p, j=j: e.scalar_tensor_tensor(out=af[p][:], in0=cf[j][:], scalar=om[:, j:j + 1], in1=af[p][:],
                                                                          op0=ALU.mult, op1=ALU.add),
                         reads=[("cf", j), ("af", p), "om"], writes=[("af", p)])
                S.dma("sp", lambda e, t=t, p=p: e.dma_start(out=XF1[2 * SEG + t * 128:2 * SEG + (t + 1) * 128, :], in_=af[p][:]),
                      reads=[("af", p)], writes=["XF1"], semkey=("afst", p))
            for bs in range(NBS):
                p = bs % 2
                for j in range(4):
                    S.dma("sp", lambda e, j=j, bs=bs: e.dma_start(out=cb[j][:], in_=XTa[(2 + j) * NBS + bs].rearrange("p k t -> p (k t)")),
                          reads=["XTall"], writes=[("cb", j)], semkey=("cb", j))
                S.op("dve", lambda e, p=p: e.tensor_scalar(out=ab[p][:], in0=cb[0][:], scalar1=om[:, 0:1], scalar2=None, op0=ALU.mult),
                     reads=[("cb", 0), "om"], writes=[("ab", p)])
                for j in range(1, 4):
                    S.op("dve", lambda e, p=p, j=j: e.scalar_tensor_tensor(out=ab[p][:], in0=cb[j][:], scalar=om[:, j:j + 1], in1=ab[p][:],
                                                                          op0=ALU.mult, op1=ALU.add),
                         reads=[("cb", j), ("ab", p), "om"], writes=[("ab", p)])
                S.dma("sp", lambda e, bs=bs, p=p: e.dma_start(out=XT1[2 * NBS + bs].rearrange("p k t -> p (k t)"), in_=ab[p][:]),
                      reads=[("ab", p)], writes=["XT1"], semkey=("abst", p))
            S.end_phase()


def _bcast_mid(ap2d, n):
    a = ap2d.ap
    return bass.AP(ap2d.tensor, ap2d.offset, [list(a[0]), [0, n]] + [list(x) for x in a[1:]])


class Prog2(Prog):
    def phase_qkv(self, XT, nseg, do_q, cs_of_tile):
        S = self.S
        i, s = self.i, self.s
        SEG, TB, NT, NBS = self.SEG, self.TB, self.NT, self.NBS
        c0 = 0 if do_q else 2048
        ncol = 2048 if do_q else 1024
        ngr = ncol // 512
        with contextlib.ExitStack() as st:
            c = self.load_consts(st, ["ident"])
            idb = c["idb"]
            W = self.sb(st, "W", [128, KC, ncol], BF16)
            for q in range(4):
                S.dma("sp", lambda e, q=q: e.dma_start(out=W[:, q * 4:(q + 1) * 4, :],
                                                       in_=s["wb_qkv"][q * 512:(q + 1) * 512, c0:c0 + ncol].rearrange("(k p) n -> p k n", p=128)),
                      reads=["wb_qkv"], writes=[("W", q)], semkey=("W", q))
            gb = self.sb(st, "gb", [128, 128], F32)
            gsw = self.sb(st, "gsw", [128, 128], F32)
            gsrc = i["att_q_gain"] if do_q else i["att_k_gain"]
            S.dma("sp", lambda e: e.dma_start(out=gb[:], in_=gsrc.partition_broadcast(128)), writes=["gb"], semkey="gb")

            def mk_gsw(e):
                ins = None
                for a_ in range(2):
                    ins = e.tensor_copy(out=gsw[:, a_ * 64:a_ * 64 + 32], in_=gb[:, a_ * 64 + 32:a_ * 64 + 64])
                    ins = e.tensor_copy(out=gsw[:, a_ * 64 + 32:a_ * 64 + 64], in_=gb[:, a_ * 64:a_ * 64 + 32])
                return ins
            S.op("dve", mk_gsw, reads=["gb"], writes=["gsw"])
            epsc = self.sb(st, "epsc", [128, 1], F32)
            S.op("pool", lambda e: e.memset(epsc[:], EPS), writes=["epsc"])
            xt = [self.sb(st, "xt", [128, KC, TB], BF16) for _ in range(2)]
            cs = [self.sb(st, "cs", [128, 2, 128], F32) for _ in range(2)]
            cg = [self.sb(st, "cg", [128, 2, 128], F32) for _ in range(2)]
            sq = [self.sb(st, "sq", [128, 512], F32) for _ in range(2)]
            ssq = [self.sb(st, "ssq", [128, 4], F32) for _ in range(2)]
            xr = [self.sb(st, "xr", [128, 4, 128], F32) for _ in range(2)]
            t1 = [self.sb(st, "t1", [128, 4, 128], F32) for _ in range(2)]
            t2 = [self.sb(st, "t2", [128, 4, 128], F32) for _ in range(2)]
            xo = [self.sb(st, "xo", [128, 4, 128], BF16) for _ in range(4)]
            vb = [self.sb(st, "vb", [128, 512], BF16) for _ in range(2)]
            nch = 16 if do_q else 4
            blk = [self.sb(st, "blk", [128, nch, TB], BF16) for _ in range(2)]
            pG = [self.ps(st, "pG", [128, 512], F32) for _ in range(3)]
            pT = [self.ps(st, "pT", [128, 8, 128], BF16) for _ in range(2)]
            ptk = [("pT", j) for j in range(2)]
            ctr = [0]
            gc = 0
            xc = [0]
            deferred = []
            for sg in range(nseg):
                for bs in range(NBS):
                    b = sg * NBS + bs
                    bb = b % 2
                    S.dma("sp", lambda e, b=b, bb=bb: e.dma_start(out=xt[bb][:], in_=XT[b]),
                          reads=[("XT", b)], writes=[("xt", bb)], semkey=("xt", bb))
                    for t in range(NT):
                        g = b * NT + t
                        p = g % 2
                        csrc = cs_of_tile(sg, bs * NT + t)
                        S.dma("sp", lambda e, p=p, csrc=csrc: e.dma_start(out=cs[p][:], in_=csrc), writes=[("cs", p)], semkey=("cs", p))
                        S.op("pool", lambda e, p=p: e.tensor_tensor(out=cg[p][:, 0, :], in0=cs[p][:, 0, :], in1=gb[:], op=ALU.mult),
                             reads=[("cs", p), "gb"], writes=[("cg", p)])
                        S.op("pool", lambda e, p=p: e.tensor_tensor(out=cg[p][:, 1, :], in0=cs[p][:, 1, :], in1=gsw[:], op=ALU.mult),
                             reads=[("cs", p), "gsw"], writes=[("cg", p)])
                        for gr in range(ngr):
                            a = gc % 3
                            q2 = gc % 2
                            gc += 1
                            self.mmg(pG[a][:, :], [(xt[bb][:, k, t * 128:(t + 1) * 128], W[:, k, gr * 512:(gr + 1) * 512]) for k in range(KC)],
                                     reads=[("xt", bb)] + [("W", q) for q in range(4)], writes=[("pG", a)])
                            is_v = (not do_q) and gr == 1
                            if is_v:
                                S.op("act", lambda e, a=a, q2=q2: e.copy(out=vb[q2][:], in_=pG[a][:, :]), reads=[("pG", a)], writes=[("vb", q2)])
                                S.dma("sp", lambda e, g=g, q2=q2: e.dma_start(out=s["VV"][g * 128:(g + 1) * 128, :], in_=vb[q2][:]),
                                      reads=[("vb", q2)], writes=[("VV", g)], semkey=("vbst", q2))
                                continue
                            S.op("act", lambda e, a=a, q2=q2: e.activation(out=sq[q2][:], in_=pG[a][:, :], func=AF.Square),
                                 reads=[("pG", a)], writes=[("sq", q2)])
                            S.op("dve", lambda e, q2=q2: e.tensor_reduce(out=ssq[q2][:], in_=sq[q2][:].rearrange("p (h d) -> p h d", d=128),
                                                                         axis=AX.X, op=ALU.add),
                                 reads=[("sq", q2)], writes=[("ssq", q2)])
                            S.op("act", lambda e, q2=q2: e.activation(out=ssq[q2][:], in_=ssq[q2][:], func=AF.Ln, bias=epsc[:], scale=1.0 / 128),
                                 reads=[("ssq", q2), "epsc"], writes=[("ssq", q2)])
                            S.op("act", lambda e, q2=q2: e.activation(out=ssq[q2][:], in_=ssq[q2][:], func=AF.Exp, scale=-0.5),
                                 reads=[("ssq", q2)], writes=[("ssq", q2)])

                            def nrm(e, a=a, q2=q2):
                                ins = None
                                for h in range(4):
                                    ins = e.tensor_scalar(out=xr[q2][:, h, :], in0=pG[a][:, h * 128:(h + 1) * 128], scalar1=ssq[q2][:, h:h + 1],
                                                          scalar2=None, op0=ALU.mult)
                                return ins
                            S.op("dve", nrm, reads=[("pG", a), ("ssq", q2)], writes=[("xr", q2)])
                            S.op("pool", lambda e, q2=q2, p=p: e.tensor_tensor(out=t1[q2][:], in0=xr[q2][:], in1=_bcast_mid(cg[p][:, 0, :], 4), op=ALU.mult),
                                 reads=[("xr", q2), ("cg", p)], writes=[("t1", q2)])

                            def rot(e, q2=q2, p=p):
                                xv = xr[q2][:].rearrange("p h (a s d) -> p h a s d", a=2, s=2)
                                tv = t2[q2][:].rearrange("p h (a s d) -> p h a s d", a=2, s=2)
                                sv = cg[p][:, 1, :].rearrange("p (a s d) -> p a s d", a=2, s=2)
                                ins = None
                                for s_ in range(2):
                                    ins = e.tensor_tensor(out=tv[:, :, :, s_, :], in0=xv[:, :, :, 1 - s_, :],
                                                          in1=_bcast_mid(sv[:, :, s_, :], 4), op=ALU.mult)
                                return ins
                            S.op("pool", rot, reads=[("xr", q2), ("cg", p)], writes=[("t2", q2)])
                            xq = xc[0] % 4
                            xc[0] += 1
                            S.op("pool", lambda e, q2=q2, xq=xq: e.tensor_tensor(out=xo[xq][:], in0=t1[q2][:], in1=t2[q2][:], op=ALU.add),
                                 reads=[("t1", q2), ("t2", q2)], writes=[("xo", xq)])
                            while len(deferred) > 2:
                                deferred.pop(0)()
                            last_in_blk = (t == NT - 1) and (gr == (ngr - 1 if do_q else 0))

                            def dtr(q2=xq, bb=bb, t=t, gr=gr, b=b, last_in_blk=last_in_blk):
                                self.transpose_tile(lambda k, q2=q2: xo[q2][:, k, :], [("xo", q2)], blk[bb], t * 128, ("blk", bb), idb, pT, ptk, ctr,
                                                    nchunks=4, dst_c0=(gr * 4 if do_q else 0))
                                if last_in_blk:
                                    if do_q:
                                        S.dma("sp", lambda e, b=b, bb=bb: e.dma_start(out=s["QT"][b], in_=blk[bb][:]),
                                              reads=[("blk", bb)], writes=[("QT", b)], semkey=("blkst", bb))
                                    else:
                                        S.dma("sp", lambda e, b=b, bb=bb: e.dma_start(out=s["KT"][:, :, b * TB:(b + 1) * TB], in_=blk[bb][:]),
                                              reads=[("blk", bb)], writes=[("KT", b)], semkey=("blkst", bb))
                            deferred.append(dtr)
            for fn_ in deferred:
                fn_()
            S.end_phase()

    def phase_attn(self, OT):
        S = self.S
        s = self.s
        SEG, TB, NBS = self.SEG, self.TB, self.NBS
        sc = 128 ** -0.5
        Lmax = 4 * SEG
        with contextlib.ExitStack() as st:
            c = self.load_consts(st, ["ones"])
            ones = c["ones"]
            ksb = [self.sb(st, "ksb", [128, Lmax], BF16) for _ in range(2)]
            vsb = [self.sb(st, "vsb", [128, Lmax // 128, 128], BF16) for _ in range(2)]
            qt = [self.sb(st, "qt", [128, KC, TB], BF16) for _ in range(2)]
            ob = [self.sb(st, "ob", [128, KC, TB], BF16) for _ in range(2)]
            PT = [self.sb(st, "PT", [128, TB], BF16) for _ in range(4)]
            rden = [self.sb(st, "rden", [128, TB], F32) for _ in range(2)]
            pS = [self.ps(st, "pS", [128, 512], F32) for _ in range(4)]
            pO = [self.ps(st, "pO", [128, 512], F32) for _ in range(2)]
            pD = [self.ps(st, "pD", [128, 512], F32) for _ in range(2)]
            steps = []
            kc = 0
            hc = 0
            for sg in range(3):
                k0 = sg * SEG if sg < 2 else 2 * SEG
                L = SEG if sg < 2 else 4 * SEG
                nkt = L // 128
                for bs in range(NBS):
                    b = sg * NBS + bs
                    bb = b % 2
                    for g in range(4):
                        kp = kc % 2
                        kc += 1
                        for j in range(4):
                            hp = hc % 2
                            hc += 1
                            for kt in range(nkt):
                                steps.append(dict(b=b, bb=bb, g=g, kp=kp, j=j, hq=g * 4 + j, hp=hp, kt=kt, nkt=nkt, k0=k0, L=L,
                                                  first_b=(g == 0 and j == 0 and kt == 0), first_g=(j == 0 and kt == 0),
                                                  last_b=(g == 3 and j == 3 and kt == nkt - 1)))

            def emit_S(i):
                d = steps[i]
                if d["first_b"]:
                    S.dma("sp", lambda e, d=d: e.dma_start(out=qt[d["bb"]][:], in_=s["QT"][d["b"]]),
                          reads=[("QT", d["b"])], writes=[("qt", d["bb"])], semkey=("qt", d["bb"]))
                if d["first_g"]:
                    S.dma("sp", lambda e, d=d: e.dma_start(out=ksb[d["kp"]][:, 0:d["L"]], in_=s["KT"][:, d["g"], d["k0"]:d["k0"] + d["L"]]),
                          reads=["KTall"], writes=[("ksb", d["kp"])], semkey=("ksb", d["kp"]))
                    S.dma("sp", lambda e, d=d: e.dma_start(
                        out=vsb[d["kp"]][:, 0:d["nkt"], :],
                        in_=s["VV"][d["k0"]:d["k0"] + d["L"], d["g"] * 128:(d["g"] + 1) * 128].rearrange("(t p) d -> p t d", p=128)),
                        reads=["VVall"], writes=[("vsb", d["kp"])], semkey=("vsb", d["kp"]))
                a = i % 4
                self.mmg(pS[a][:, 0:TB], [(ksb[d["kp"]][:, d["kt"] * 128:(d["kt"] + 1) * 128], qt[d["bb"]][:, d["hq"], :])],
                         reads=[("ksb", d["kp"]), ("qt", d["bb"])], writes=[("pS", a)])

            LOOK = 2
            for i in range(min(LOOK, len(steps))):
                emit_S(i)
            for i, d in enumerate(steps):
                if i + LOOK < len(steps):
                    emit_S(i + LOOK)
                a = i % 4
                hp, kt, nkt, kp, bb, hq = d["hp"], d["kt"], d["nkt"], d["kp"], d["bb"], d["hq"]
                S.op("act", lambda e, a=a: e.activation(out=PT[a][:], in_=pS[a][:, 0:TB], func=AF.Exp, scale=sc),
                     reads=[("pS", a)], writes=[("PT", a)])
                self.mmg(pO[hp][:, 0:TB], [(vsb[kp][:, kt, :], PT[a][:])], reads=[("vsb", kp), ("PT", a)], writes=[("pO", hp)],
                         first=(kt == 0), last=(kt == nkt - 1))
                self.mmg(pD[hp][:, 0:TB], [(ones[:], PT[a][:])], reads=["ones", ("PT", a)], writes=[("pD", hp)],
                         first=(kt == 0), last=(kt == nkt - 1))
                if kt == nkt - 1:
                    S.op("dve", lambda e, hp=hp: e.reciprocal(out=rden[hp][:], in_=pD[hp][:, 0:TB]),
                         reads=[("pD", hp)], writes=[("rden", hp)])
                    S.op("dve", lambda e, hp=hp, hq=hq, bb=bb: e.tensor_tensor(out=ob[bb][:, hq, :], in0=pO[hp][:, 0:TB], in1=rden[hp][:], op=ALU.mult),
                         reads=[("pO", hp), ("rden", hp)], writes=[("ob", bb)])
                if d["last_b"]:
                    S.dma("pool", lambda e, d=d: e.dma_start(out=OT[d["b"]], in_=ob[d["bb"]][:]),
                          reads=[("ob", d["bb"])], writes=[("OT", d["b"])], semkey=("obst", d["bb"]))
            S.end_phase()


class Prog3(Prog2):
    def phase_gla(self, XT, OT, seqs):
        S = self.S
        i, s = self.i, self.s
        TB, NT = self.TB, self.NT
        wg = s["wb_gin"]
        with contextlib.ExitStack() as st:
            c = self.load_consts(st, ["ident"])
            idb = c["idb"]
            cn = {}
            for nm, shp in (("trif", [128, 130]), ("trib", [128, 130]), ("trikf", [128, 128]), ("trikb", [128, 128]),
                            ("maskf", [128, 128]), ("maskb", [128, 128])):
                cn[nm] = self.sb(st, nm, shp, F32)
                S.dma("sp", lambda e, nm=nm: e.dma_start(out=cn[nm][:], in_=i[nm]), writes=[nm], semkey=nm)
            m2 = self.sb(st, "m2", [128, 2, 128], F32)
            S.dma("sp", lambda e: e.dma_start(out=m2[:, 0, :], in_=i["maskf"]), writes=["m2"], semkey="m2a")
            S.dma("sp", lambda e: e.dma_start(out=m2[:, 1, :], in_=i["maskb"]), writes=["m2"], semkey="m2b")
            wup = self.sb(st, "wup", [17, 2, 1024], BF16)
            for d_ in range(2):
                S.dma("pool", lambda e, d_=d_: e.dma_start(out=wup[0:16, d_, :], in_=i["gla_w_gate_up"][d_]), writes=["wup"], semkey=("wup", d_))
                S.dma("pool", lambda e, d_=d_: e.dma_start(out=wup[16:17, d_, :], in_=i["gla_b_gate"][d_:d_ + 1, :]), writes=["wup"], semkey=("wupb", d_))
            ngb = self.sb(st, "ngb", [128, 512], F32)
            S.dma("sp", lambda e: e.dma_start(out=ngb[:], in_=i["gla_norm_g"].partition_broadcast(128)), writes=["ngb"], semkey="ngb")
            epsc = self.sb(st, "epsc", [128, 1], F32)
            S.op("pool", lambda e: e.memset(epsc[:], EPS), writes=["epsc"])
            wqk = self.sb(st, "wqk", [128, KC, 512], BF16)
            wv = self.sb(st, "wv", [128, KC, 512], BF16)
            wr = self.sb(st, "wr", [128, KC, 512], BF16)
            wz = self.sb(st, "wz", [128, KC, 32], BF16)
            S.dma("sp", lambda e: e.dma_start(out=wz[:], in_=wg[:, 6144:6176].rearrange("(k p) n -> p k n", p=128)),
                  reads=["wb_gin"], writes=["wz"], semkey="wz")
            xt = [self.sb(st, "xt", [128, KC, TB], BF16) for _ in range(2)]
            zT = [self.sb(st, "zT", [17, 2, TB], BF16) for _ in range(2)]
            for z_ in range(2):
                S.op("pool", lambda e, z_=z_: e.memset(zT[z_][:], 1.0), writes=[("zT", z_)])
            e1 = self.sb(st, "e1", [128, 2, 256], F32)
            sp = [self.sb(st, "sp", [128, 2, 256], F32) for _ in range(3)]
            EK = self.sb(st, "EK", [128, 2, 256], F32)
            ktmp = [self.sb(st, "ktmp", [128, 256], BF16) for _ in range(2)]
            ET = self.sb(st, "ET", [128, 2, 2, 128], F32)
            EiT = self.sb(st, "EiT", [128, 2, 2, 128], F32)
            dec = [self.sb(st, "dec", [128, 4], F32) for _ in range(3)]
            qT = [self.sb(st, "qT", [128, 2, TB], F32) for _ in range(2)]
            kT = [self.sb(st, "kT", [128, 2, TB], F32) for _ in range(2)]
            qe = [self.sb(st, "qe", [128, 2, 2, 128], BF16) for _ in range(3)]
            ke = [self.sb(st, "ke", [128, 2, 2, 128], BF16) for _ in range(3)]
            kv = [self.sb(st, "kv", [128, 768], BF16) for _ in range(3)]
            kd = [self.sb(st, "kd", [128, 256], BF16) for _ in range(3)]
            rs = [self.sb(st, "rs", [128, 512], F32) for _ in range(3 * NT)]
            Pm = [self.sb(st, "Pm", [128, 2, 128], BF16) for _ in range(2)]
            S32 = self.sb(st, "S32", [128, 2, 512], F32)
            Sb = [self.sb(st, "Sb", [128, 2, 512], BF16) for _ in range(4)]
            SbL = [self.sb(st, "SbL", [128, 2, 2, 512], BF16) for _ in range(3)]
            junk = self.sb(st, "junk", [128, 512], F32)
            ssq = self.sb(st, "ssq", [128, 1], F32)
            on1 = self.sb(st, "on1", [128, 512], F32)
            on2 = [self.sb(st, "on2", [128, 512], BF16) for _ in range(2)]
            oblk = [self.sb(st, "oblk", [128, 4, TB], BF16) for _ in range(2)]
            pg = self.ps(st, "pg", [128, 512], F32)
            pc = self.ps(st, "pc", [128, 2, 2, 128], F32)
            pka = self.ps(st, "pka", [128, 512], F32)
            po = self.ps(st, "po", [128, 512], F32)
            pu = [self.ps(st, "pu", [128, 512], F32) for _ in range(2)]
            pP = [self.ps(st, "pP", [128, 512], F32) for _ in range(2)]
            pTt = [pP[0][:].bitcast(BF16), pP[1][:].bitcast(BF16)]
            tri = {0: cn["trif"], 1: cn["trib"]}
            trik = {0: cn["trikf"], 1: cn["trikb"]}
            KVst = s["KVst"]
            st_ = {"pp": 0, "sb": 0}

            def nextpp():
                a = st_["pp"] % 2
                st_["pp"] += 1
                return a

            def s_cast():
                q = st_["sb"] % 4
                st_["sb"] += 1
                S.op("act", lambda e, q=q: e.copy(out=Sb[q][:], in_=S32[:]), reads=["S32"], writes=[("Sb", q)])
                return q

            def run_skewed(n, stages):
                maxlead = max(l for l, _ in stages)
                minlead = min(l for l, _ in stages)
                for it in range(-maxlead, n - minlead):
                    for lead, fn in stages:
                        g = it + lead
                        if 0 <= g < n:
                            fn(g)

            for (b0, nb) in seqs:
                ntile = nb * NT
                for h in range(4):
                    S.dma("sp", lambda e, h=h: e.dma_start(out=wqk[:, :, 0:256], in_=wg[:, h * 256:(h + 1) * 256].rearrange("(k p) n -> p k n", p=128)),
                          reads=["wb_gin"], writes=["wqk"], semkey="wqk")
                    S.dma("sp", lambda e, h=h: e.dma_start(out=wqk[:, :, 256:512], in_=wg[:, 1024 + h * 256:1024 + (h + 1) * 256].rearrange("(k p) n -> p k n", p=128)),
                          reads=["wb_gin"], writes=["wqk"], semkey="wqk2")
                    S.dma("sp", lambda e, h=h: e.dma_start(out=wv[:], in_=wg[:, 2048 + h * 512:2048 + (h + 1) * 512].rearrange("(k p) n -> p k n", p=128)),
                          reads=["wb_gin"], writes=["wv"], semkey="wv")
                    S.dma("sp", lambda e, h=h: e.dma_start(out=wr[:], in_=wg[:, 4096 + h * 512:4096 + (h + 1) * 512].rearrange("(k p) n -> p k n", p=128)),
                          reads=["wb_gin"], writes=["wr"], semkey="wr")

                    S.op("pool", lambda e: e.memset(S32[:], 0.0), writes=["S32"])

                    def btile(j):
                        g = ntile - 1 - j
                        return g // NT, g % NT, g

                    def B_blk(j, dirs):
                        bl, t, g = btile(j) if dirs == [1, 0] else (j // NT, j % NT, j)
                        first = (t == NT - 1) if dirs == [1, 0] else (t == 0)
                        if not first:
                            return
                        bb = bl % 2
                        S.dma("sp", lambda e, bl=bl, bb=bb, b0=b0: e.dma_start(out=xt[bb][:], in_=XT[b0 + bl]), reads=[("XT", b0 + bl)], writes=[("xt", bb)], semkey=("xt", bb))
                        for d_ in dirs:
                            a = nextpp()
                            self.mmg(pP[a][0:16, 0:TB], [(wz[:, k, d_ * 16:(d_ + 1) * 16], xt[bb][:, k, :]) for k in range(KC)],
                                     reads=["wz", ("xt", bb)], writes=[("pP", a)])
                            S.op("act", lambda e, d_=d_, a=a, bb=bb: e.copy(out=zT[bb][0:16, d_, :], in_=pP[a][0:16, 0:TB]), reads=[("pP", a)], writes=[("zT", bb)])

                    def gates(bl, t, g, dirs):
                        bb = bl % 2
                        s3 = g % 3
                        for d_ in dirs:
                            self.mmg(pg[:, d_ * 256:(d_ + 1) * 256], [(zT[bb][:, d_, t * 128:(t + 1) * 128], wup[:, d_, h * 256:(h + 1) * 256])],
                                     reads=[("zT", bb), "wup"], writes=["pg"])
                        lo, hi = dirs[0], dirs[-1] + 1
                        S.op("act", lambda e: e.activation(out=e1[:, lo:hi, :], in_=pg[:, lo * 256:hi * 256].rearrange("p (a b) -> p a b", b=256), func=AF.Exp, scale=-1.0),
                             reads=["pg"], writes=["e1"])
                        S.op("act", lambda e, s3=s3: e.activation(out=sp[s3][:, lo:hi, :], in_=e1[:, lo:hi, :], func=AF.Ln, bias=1.0, scale=1.0),
                             reads=["e1"], writes=[("sp", s3)])

                    def bT1(j):
                        B_blk(j, [1, 0])
                        bl, t, g = btile(j)
                        gates(bl, t, g, [0, 1])

                    def bT2(j):
                        bl, t, g = btile(j)
                        bb = bl % 2
                        s3 = g % 3
                        for d_ in range(2):
                            self.mmg(pka[:, d_ * 256:(d_ + 1) * 256], [(trik[d_][:], sp[s3][:, d_, :])], reads=["trikf", "trikb", ("sp", s3)], writes=["pka"])
                        S.op("act", lambda e: e.activation(out=EK[:].rearrange("p a b -> p (a b)"), in_=pka[:, 0:512], func=AF.Exp), reads=["pka"], writes=["EK"])
                        a = nextpp()
                        kq = g % 2
                        self.mmg(pP[a][:, 0:256], [(xt[bb][:, k, t * 128:(t + 1) * 128], wqk[:, k, 256:512]) for k in range(KC)],
                                 reads=[("xt", bb), "wqk"], writes=[("pP", a)])
                        S.op("act", lambda e, a=a, kq=kq: e.copy(out=ktmp[kq][:], in_=pP[a][:, 0:256]), reads=[("pP", a)], writes=[("ktmp", kq)])
                        S.op("dve", lambda e, s3=s3, kq=kq: e.tensor_tensor(out=kd[s3][:], in0=ktmp[kq][:], in1=EK[:, 1, :], op=ALU.mult),
                             reads=[("ktmp", kq), "EK"], writes=[("kd", s3)])
                        S.op("dve", lambda e, s3=s3, kq=kq: e.tensor_tensor(out=kv[s3][:, 0:256], in0=ktmp[kq][:], in1=EK[:, 0, :], op=ALU.mult),
                             reads=[("ktmp", kq), "EK"], writes=[("kvk", s3)])
                        a2 = nextpp()
                        self.mmg(pP[a2][:, :], [(xt[bb][:, k, t * 128:(t + 1) * 128], wv[:, k, :]) for k in range(KC)],
                                 reads=[("xt", bb), "wv"], writes=[("pP", a2)])
                        S.op("act", lambda e, a2=a2, s3=s3: e.copy(out=kv[s3][:, 256:768], in_=pP[a2][:, :]), reads=[("pP", a2)], writes=[("kvv", s3)])
                        S.dma("sp", lambda e, g=g, s3=s3: e.dma_start(out=KVst[g], in_=kv[s3][:]), reads=[("kvk", s3), ("kvv", s3)], writes=[("KVst", g)],
                              semkey=("kvst", s3))
                        pcf = pc[:].rearrange("p a b c -> p (a b c)")
                        for dk in range(2):
                            self.mmg(pcf[:, dk * 2:dk * 2 + 2], [(sp[s3][:, 1, dk * 128:(dk + 1) * 128], tri[1][:, 128:130])],
                                     reads=[("sp", s3), "trib"], writes=["pc"])
                        S.op("act", lambda e, s3=s3, pcf=pcf: e.activation(out=dec[s3][:], in_=pcf[:, 0:4], func=AF.Exp), reads=["pc"], writes=[("dec", s3)])

                    def bT3(j):
                        bl, t, g = btile(j)
                        s3 = g % 3
                        for ci in (1, 0):
                            cidx = g * 2 + ci
                            q = s_cast()
                            S.dma("sp", lambda e, q=q, cidx=cidx: e.dma_start(out=s["SBst"][cidx], in_=Sb[q][:]),
                                  reads=[("Sb", q)], writes=[("SBst", cidx)], semkey=("sbst", q))
                            for dk in range(2):
                                self.mmg(pu[dk][:, :], [(kd[s3][ci * 64:(ci + 1) * 64, dk * 128:(dk + 1) * 128], kv[s3][ci * 64:(ci + 1) * 64, 256:768])],
                                         reads=[("kd", s3), ("kvv", s3)], writes=[("pu", dk)])
                            for dk in range(2):
                                S.op("dve", lambda e, dk=dk, ci=ci, s3=s3: e.scalar_tensor_tensor(
                                    out=S32[:, dk, :], in0=S32[:, dk, :], scalar=dec[s3][:, dk * 2 + ci:dk * 2 + ci + 1],
                                    in1=pu[dk][:, :], op0=ALU.mult, op1=ALU.add),
                                    reads=[("pu", dk), ("dec", s3), "S32"], writes=["S32"])

                    import os as _os
                    run_skewed(ntile, [(2, bT1), (1, bT2), (0, bT3)][:int(_os.environ.get("GLA_B", "3"))])

                    S.op("pool", lambda e: e.memset(S32[:], 0.0), writes=["S32"])
                    fst = {"q": s_cast()}

                    def fB_tasks(bl):
                        bb = bl % 2
                        tasks = []

                        def t_load():
                            S.dma("sp", lambda e, bl=bl, bb=bb, b0=b0: e.dma_start(out=xt[bb][:], in_=XT[b0 + bl]), reads=[("XT", b0 + bl)], writes=[("xt", bb)], semkey=("xt", bb))
                            for d_ in (0, 1):
                                a = nextpp()
                                self.mmg(pP[a][0:16, 0:TB], [(wz[:, k, d_ * 16:(d_ + 1) * 16], xt[bb][:, k, :]) for k in range(KC)],
                                         reads=["wz", ("xt", bb)], writes=[("pP", a)])
                                S.op("act", lambda e, d_=d_, a=a, bb=bb: e.copy(out=zT[bb][0:16, d_, :], in_=pP[a][0:16, 0:TB]), reads=[("pP", a)], writes=[("zT", bb)])
                        tasks.append(t_load)
                        for (dst, key, c0) in ((qT, "qT", 0), (kT, "kT", 256)):
                            for dk in range(2):
                                def t_qk(dst=dst, key=key, c0=c0, dk=dk):
                                    a = nextpp()
                                    self.mmg(pP[a][:, 0:TB], [(wqk[:, k, c0 + dk * 128:c0 + (dk + 1) * 128], xt[bb][:, k, :]) for k in range(KC)],
                                             reads=["wqk", ("xt", bb)], writes=[("pP", a)])
                                    S.op("act", lambda e, a=a, dst=dst, dk=dk, bb=bb: e.copy(out=dst[bb][:, dk, :], in_=pP[a][:, 0:TB]),
                                         reads=[("pP", a)], writes=[(key, bb)])
                                tasks.append(t_qk)
                        for t2 in range(NT):
                            def t_r(t2=t2):
                                a = nextpp()
                                ri = (bl % 3) * NT + t2
                                self.mmg(pP[a][:, :], [(xt[bb][:, k, t2 * 128:(t2 + 1) * 128], wr[:, k, :]) for k in range(KC)],
                                         reads=[("xt", bb), "wr"], writes=[("pP", a)])
                                S.op("act", lambda e, a=a, ri=ri: e.activation(out=rs[ri][:], in_=pP[a][:, :], func=AF.Silu),
                                     reads=[("pP", a)], writes=[("rs", ri)])
                            tasks.append(t_r)
                        return tasks

                    nblk_f = ntile // NT
                    spread = (NT == 4 and nblk_f > 1)
                    sched_tasks = {}
                    if spread:
                        for bl in range(1, nblk_f):
                            tk = fB_tasks(bl)
                            g0 = (bl - 1) * NT
                            sched_tasks[g0] = [tk[0], tk[1]]
                            sched_tasks[g0 + 1] = [tk[2], tk[3]]
                            sched_tasks[g0 + 2] = [tk[4], tk[5]]
                            sched_tasks[g0 + 3] = [tk[6], tk[7], tk[8]]

                    def fB(g):
                        bl, t = g // NT, g % NT
                        if t != 0:
                            return
                        if spread and bl > 0:
                            return
                        for fn_ in fB_tasks(bl):
                            fn_()

                    def fBs(g):
                        for fn_ in sched_tasks.get(g, ()):
                            fn_()

                    def fS1(g):
                        fB(g)
                        gates(g // NT, g % NT, g, [0, 1])

                    def fS2(g):
                        bl, t = g // NT, g % NT
                        bb = bl % 2
                        s3 = g % 3
                        tsl = slice(t * 128, (t + 1) * 128)
                        S.dma("sp", lambda e, g=g, s3=s3: e.dma_start(out=kv[s3][:], in_=KVst[g]), reads=[("KVst", g)], writes=[("kvk", s3), ("kvv", s3)],
                              semkey=("kvld", s3))
                        S.dma("sp", lambda e, s3=s3, g=g: e.dma_start(out=SbL[s3][:], in_=s["SBst"][g * 2:g * 2 + 2].rearrange("c p a b -> p c a b")),
                              reads=[("SBst", g * 2), ("SBst", g * 2 + 1)], writes=[("SbL", s3)], semkey=("sbl", s3))
                        for d_ in range(2):
                            for dk in range(2):
                                self.mmg(pc[:, d_, dk, :], [(sp[s3][:, d_, dk * 128:(dk + 1) * 128], tri[d_][:, 0:128])],
                                         reads=[("sp", s3), "trif", "trib"], writes=["pc"])
                        S.op("act", lambda e: e.activation(out=ET[:].rearrange("p a b c -> p (a b c)"), in_=pc[:].rearrange("p a b c -> p (a b c)"), func=AF.Exp),
                             reads=["pc"], writes=["ET"])
                        S.op("act", lambda e: e.activation(out=EiT[:].rearrange("p a b c -> p (a b c)"), in_=pc[:].rearrange("p a b c -> p (a b c)"), func=AF.Exp, scale=-1.0),
                             reads=["pc"], writes=["EiT"])

                        def mkdec(e, s3=s3):
                            ins = None
                            for dk in range(2):
                                for ci in range(2):
                                    ins = e.tensor_copy(out=dec[s3][:, dk * 2 + ci:dk * 2 + ci + 1], in_=ET[:, 0, dk, ci * 64 + 63:ci * 64 + 64])
                            return ins
                        S.op("dve", mkdec, reads=["ET"], writes=[("dec", s3)])
                        for d_ in range(2):
                            S.op("dve", lambda e, d_=d_, s3=s3, tsl=tsl, bb=bb: e.scalar_tensor_tensor(
                                out=qe[s3][:, d_, :, :], in0=qT[bb][:, :, tsl], scalar=1.0 / 16, in1=ET[:, d_, :, :], op0=ALU.mult, op1=ALU.mult),
                                reads=[("qT", bb), "ET"], writes=[("qe", s3)])
                            S.op("pool", lambda e, d_=d_, s3=s3, tsl=tsl, bb=bb: e.tensor_tensor(out=ke[s3][:, d_, :, :], in0=kT[bb][:, :, tsl], in1=EiT[:, d_, :, :], op=ALU.mult),
                                 reads=[("kT", bb), "EiT"], writes=[("ke", s3)])

                    def fS3(g):
                        s3 = g % 3
                        p2 = g % 2
                        for d_ in range(2):
                            self.mmg(pka[:, d_ * 128:(d_ + 1) * 128], [(ke[s3][:, d_, dk, :], qe[s3][:, d_, dk, :]) for dk in range(2)],
                                     reads=[("ke", s3), ("qe", s3)], writes=["pka"])
                        S.op("dve", lambda e, p2=p2: e.tensor_tensor(out=Pm[p2][:].rearrange("p a b -> p (a b)"), in0=pka[:, 0:256],
                                                                     in1=m2[:].rearrange("p a b -> p (a b)"), op=ALU.mult),
                             reads=["pka", "m2"], writes=[("Pm", p2)])

                    def s_upd(s3, ci):
                        for dk in range(2):
                            S.op("dve", lambda e, dk=dk, ci=ci, s3=s3: e.scalar_tensor_tensor(
                                out=S32[:, dk, :], in0=S32[:, dk, :], scalar=dec[s3][:, dk * 2 + ci:dk * 2 + ci + 1],
                                in1=pu[dk][:, :], op0=ALU.mult, op1=ALU.add),
                                reads=[("pu", dk), ("dec", s3), "S32"], writes=["S32"])

                    def u_mm(s3, ci):
                        for dk in range(2):
                            self.mmg(pu[dk][:, :], [(kv[s3][ci * 64:(ci + 1) * 64, dk * 128:(dk + 1) * 128], kv[s3][ci * 64:(ci + 1) * 64, 256:768])],
                                     reads=[("kvk", s3), ("kvv", s3)], writes=[("pu", dk)])

                    def fS4a(g):
                        s3 = g % 3
                        p2 = g % 2
                        q0 = fst["q"]
                        u_mm(s3, 0)
                        self.mmg(po[:, :], [(Pm[p2][:, 0, :], kv[s3][:, 256:768]), (Pm[p2][:, 1, :], kv[s3][:, 256:768])],
                                 reads=[("Pm", p2), ("kvv", s3)], writes=["po"], first=True, last=False)
                        for ci in range(2):
                            csl = slice(ci * 64, (ci + 1) * 64)
                            self.mmg(po[csl, :], [(qe[s3][:, 1, dk, csl], SbL[s3][:, ci, dk, :]) for dk in range(2)],
                                     reads=[("qe", s3), ("SbL", s3)], writes=["po"], first=False, last=False)
                        self.mmg(po[0:64, :], [(qe[s3][:, 0, dk, 0:64], Sb[q0][:, dk, :]) for dk in range(2)],
                                 reads=[("qe", s3), ("Sb", q0)], writes=["po"], first=False, last=False)
                        s_upd(s3, 0)
                        fst["q1"] = s_cast()

                    def fS4b(g):
                        s3 = g % 3
                        p2 = g % 2
                        q1 = fst["q1"]
                        u_mm(s3, 1)
                        self.mmg(po[64:128, :], [(qe[s3][:, 0, dk, 64:128], Sb[q1][:, dk, :]) for dk in range(2)],
                                 reads=[("qe", s3), ("Sb", q1)], writes=["po"], first=False, last=True)
                        s_upd(s3, 1)
                        fst["q"] = s_cast()
                        bl, t = g // NT, g % NT
                        ri = (bl % 3) * NT + t
                        S.op("act", lambda e: e.activation(out=junk[:], in_=po[:, :], func=AF.Square, accum_out=ssq[:]),
                             reads=["po"], writes=["junk", "ssq"])
                        S.op("act", lambda e: e.activation(out=ssq[:], in_=ssq[:], func=AF.Ln, bias=epsc[:], scale=1.0 / 512),
                             reads=["ssq", "epsc"], writes=["ssq"])
                        S.op("act", lambda e: e.activation(out=ssq[:], in_=ssq[:], func=AF.Exp, scale=-0.5), reads=["ssq"], writes=["ssq"])
                        S.op("dve", lambda e: e.scalar_tensor_tensor(out=on1[:], in0=po[:, :], scalar=ssq[:], in1=ngb[:], op0=ALU.mult, op1=ALU.mult),
                             reads=["po", "ssq", "ngb"], writes=["on1"])
                        S.op("pool", lambda e, p2=p2, ri=ri: e.tensor_tensor(out=on2[p2][:], in0=on1[:], in1=rs[ri][:], op=ALU.mult),
                             reads=["on1", ("rs", ri)], writes=[("on2", p2)])

                    def fS5(g):
                        bl, t = g // NT, g % NT
                        p2 = g % 2
                        ob_ = bl % 2
                        a = nextpp()

                        def tp(e, a=a, p2=p2):
                            ins = None
                            for k in range(4):
                                ins = e.transpose(out=pTt[a][:, k * 128:(k + 1) * 128], in_=on2[p2][:, k * 128:(k + 1) * 128], identity=idb[:])
                            return ins
                        S.op("pe", tp, reads=[("on2", p2), "idb"], writes=[("pP", a)])
                        S.op("act", lambda e, a=a, ob_=ob_, t=t: e.copy(out=oblk[ob_][:, :, t * 128:(t + 1) * 128],
                                                                        in_=pTt[a][:, 0:512].rearrange("p (k t) -> p k t", t=128)),
                             reads=[("pP", a)], writes=[("oblk", ob_)])
                        if t == NT - 1:
                            b = b0 + bl
                            S.dma("sp", lambda e, b=b, ob_=ob_, h=h: e.dma_start(out=OT[b][:, h * 4:(h + 1) * 4, :], in_=oblk[ob_][:]),
                                  reads=[("oblk", ob_)], writes=[("OT", b)], semkey=("oblkst", ob_))

                    import os as _os
                    _m = _os.environ.get("GLA_DBG", "")
                    if _m == "bwd":
                        continue
                    _stg = [(0, fBs), (0, fS4a), (3, fS1), (1, fS3), (0, fS4b), (2, fS2), (-1, fS5)]
                    if _m.startswith("n"):
                        _stg = _stg[:int(_m[1:])]
                    run_skewed(ntile, _stg)
            S.end_phase()

    def build(self, upto=99):
        SEG, NBS = self.SEG, self.NBS
        NA = self.NSA * SEG
        NB_ = self.NSB * SEG
        with contextlib.ExitStack() as gst:
            self.declare()
            self.S = Sched(self.nc, gst)
            i, s = self.i, self.s
            self.phase_weights(0)
            self.phase_prep(i["xA"], s["XTa"], NA)
            self.phase_weights(1)
            self.phase_memkv()
            seqs = [(0, NBS), (NBS, NBS), (2 * NBS, 4 * NBS)]
            memA = [0, 1, 2, 2, 2, 2]
            memB = [0, 1, 2]
            self.phase_gla(s["XTa"], s["OT"], seqs)
            if upto == -1:
                S = self.S
                dbg2 = self.dout("dbgOT", [NA // self.TB, 128, KC, self.TB], BF16)
                S.dma("sp", lambda e: e.dma_start(out=dbg2, in_=s["OT"]), semkey="dump")
                S.dma("sp", lambda e: e.dma_start(out=self.y[0:128, :], in_=i["xA"][0:128, :]), semkey="dump2")
                S.end_phase()
                return
            self.phase_proj_ln(s["wb_gout"], "wb_gout", s["OT"], i["xA"], s["XFa"], s["XTb"], NA, 0, 0)
            self.phase_xattn(0, s["XTb"], s["OT"], 6, memA)
            self.phase_proj_ln(s["wb_mo"][0], "wb_mo", s["OT"], s["XFa"], s["XFb"], s["XTa"], NA, 0, 1)
            self.phase_mlp(0, s["XTa"], s["XFb"], s["XFa"], s["XTb"], NA)
            if upto == 0:
                self._dump(s["XFa"], NA)
                return
            csA = i["csA"]
            self.phase_qkv(s["XTb"], 6, False, lambda sg, tt: csA[(0 if sg < 2 else (sg - 2) * SEG) + tt * 128:(0 if sg < 2 else (sg - 2) * SEG) + (tt + 1) * 128])
            self.phase_select(s["XFa"], s["XTb"], s["XF1"], s["XT1"])
            self.phase_qkv(s["XT1"], 3, True, lambda sg, tt: (csA if sg < 2 else i["csOwn"])[tt * 128:(tt + 1) * 128])
            self.phase_attn(s["OT"])
            self.phase_proj_ln(s["wb_aout"], "wb_aout", s["OT"], s["XF1"], s["XFb"], s["XTa"], NB_, 1, 0)
            self.phase_xattn(1, s["XTa"], s["OT"], 3, memB)
            self.phase_proj_ln(s["wb_mo"][1], "wb_mo", s["OT"], s["XFb"], s["XFa"], s["XT1"], NB_, 1, 1)
            self.phase_mlp(1, s["XT1"], s["XFa"], None, None, NB_, final_out=self.y)

    def _dump(self, src, n):
        S = self.S
        S.dma("sp", lambda e: e.dma_start(out=self.dbgout[0:n, :], in_=src[0:n, :]), semkey="dump")
        S.dma("sp", lambda e: e.dma_start(out=self.y[:, :], in_=src[0:self.NSB * self.SEG, :]), semkey="dump2")
        S.end_phase()


def _consts(SEG):
    j = np.arange(128)[:, None]
    t = np.arange(128)[None, :]
    same = (j // 64) == (t // 64)
    g = -1.0 / 16.0
    c = {}
    trif = np.zeros((128, 130), np.float32)
    trib = np.zeros((128, 130), np.float32)
    trif[:, :128] = g * (same & (j <= t))
    trib[:, :128] = g * (same & (j >= t))
    for cc in range(2):
        trif[:, 128 + cc] = g * ((np.arange(128) // 64) == cc)
        trib[:, 128 + cc] = g * ((np.arange(128) // 64) == cc)
    c["trif"], c["trib"] = trif, trib
    c["trikf"] = (g * (same & (j > t))).astype(np.float32)
    c["trikb"] = (g * (same & (j < t))).astype(np.float32)
    c["maskf"] = (same & (j <= t)).astype(np.float32)
    c["maskb"] = (same & (j > t)).astype(np.float32)
    c["ident"] = np.eye(128, dtype=np.float32)
    pos = np.arange(4 * SEG)
    row = (pos // 64).astype(np.float32)
    col = (pos % 64).astype(np.float32)
    inv = (10000.0 ** (-np.arange(0, 64, 2, dtype=np.float32) / 64.0)).astype(np.float32)
    ar = row[:, None] * inv
    ac = col[:, None] * inv
    ang = np.concatenate([ar, ar, ac, ac], axis=-1).astype(np.float32)
    sign = np.concatenate([-np.ones(32), np.ones(32), -np.ones(32), np.ones(32)]).astype(np.float32)
    cs = np.stack([np.cos(ang), np.sin(ang) * sign], axis=1).astype(np.float32)
    c["csA"] = np.ascontiguousarray(cs)
    return c


_CACHE = {}


def _get_prog(SEG, upto=99, dbg=False):
    key = (SEG, upto, dbg)
    if key not in _CACHE:
        p = Prog3(SEG, dbg=dbg)
        p.build(upto=upto)
        _CACHE[key] = p
    return _CACHE[key]


def kernel(x_prompt, x_sample, mem_prompt, mem_sample, gla_w_in, gla_w_gate_up, gla_b_gate, gla_norm_g, gla_w_out,
           att_w_qkv, att_q_gain, att_k_gain, att_w_out, mem_w_q, mem_w_kv, mem_w_o, mlp_w1, mlp_w2, ln_g, ln_b,
           _upto=99, _dbg=False):
    f = lambda a: np.ascontiguousarray(np.asarray(a, dtype=np.float32))
    x_prompt, x_sample, mem_prompt, mem_sample = f(x_prompt), f(x_sample), f(mem_prompt), f(mem_sample)
    SEG = x_prompt.shape[1]
    assert x_prompt.shape[0] == 16 and x_sample.shape[0] == 2 and x_sample.shape[1] == 4 * SEG
    prog = _get_prog(SEG, _upto, _dbg)
    cst = _consts(SEG)
    shared = {
        "gla_w_in": f(gla_w_in)[0], "gla_w_gate_up": f(gla_w_gate_up)[0], "gla_b_gate": f(gla_b_gate)[0],
        "gla_norm_g": f(gla_norm_g), "gla_w_out": f(gla_w_out)[0], "att_w_qkv": f(att_w_qkv)[0],
        "att_q_gain": f(att_q_gain), "att_k_gain": f(att_k_gain), "att_w_out": f(att_w_out)[0],
        "mem_w_q": f(mem_w_q), "mem_w_kv": f(mem_w_kv), "mem_w_o": f(mem_w_o), "mlp_w1": f(mlp_w1), "mlp_w2": f(mlp_w2),
        "ln_g": f(ln_g), "ln_b": f(ln_b),
    }
    for k in ("ident", "trif", "trib", "trikf", "trikb", "maskf", "maskb", "csA"):
        shared[k] = cst[k]
    in_maps = []
    for c in range(8):
        sq, qt = c // 4, c % 4
        m = dict(shared)
        m["xA"] = np.ascontiguousarray(np.concatenate([x_prompt[2 * c], x_prompt[2 * c + 1], x_sample[sq]], axis=0))
        m["memA"] = np.ascontiguousarray(np.concatenate([mem_prompt[2 * c], mem_prompt[2 * c + 1], mem_sample[sq]], axis=0))
        m["csOwn"] = np.ascontiguousarray(cst["csA"][qt * SEG:(qt + 1) * SEG])
        om = np.zeros((128, 4), np.float32)
        om[:, qt] = 1.0
        m["ownmask"] = om
        in_maps.append(m)
    res = run_bass_kernel_spmd(prog.nc, in_maps, core_ids=list(range(8)))
    y_prompt = np.zeros((16, SEG, D), np.float32)
    y_sample = np.zeros((2, 4 * SEG, D), np.float32)
    for c in range(8):
        y = res.results[c]["y"]
        y_prompt[2 * c] = y[0:SEG]
        y_prompt[2 * c + 1] = y[SEG:2 * SEG]
        y_sample[c // 4, (c % 4) * SEG:(c % 4 + 1) * SEG] = y[2 * SEG:3 * SEG]
    if _dbg:
        return (y_prompt, y_sample), [r["dbg"] for r in res.results]
    return (y_prompt, y_sample)
```
